# Optimizing a Trainium2 kernel written in Bass

```python
import math
import jax, jax.numpy as jnp
from jax import lax
import numpy as np

D_MODEL = 1024
BATCH = 16
SEQ = 256
DEPTH = 2
DEC_BATCH = 8
DEC_SEQ = 4096
PAST_LEN = 512

GRID_W = 64
ROPE_BASE = 10000.0
NORM_EPS = 1e-6
Q_BLOCK = 128

N_AB = (DEPTH + 1) // 2
N_C = DEPTH // 2

MLA_HEADS = 8
MLA_NOPE = 64
MLA_ROPE = 32
MLA_V = 64
MLA_Q_LORA = 384
MLA_KV_LORA = 256
GLA_HEADS = 4
GLA_DK = 128
GLA_DV = 128
GLA_GATE_RANK = 16
GLA_GATE_NORM = 16.0
GLA_CHUNK = 64
DIFF_HEADS = 8
DIFF_HEAD_DIM = 64
FFN_HIDDEN = -(-8 * D_MODEL // (3 * 256)) * 256

AB_SIZES = (MLA_Q_LORA, MLA_KV_LORA, MLA_ROPE, GLA_HEADS * GLA_DK, GLA_HEADS * GLA_DK,
            GLA_HEADS * GLA_DV, 2 * GLA_GATE_RANK, GLA_HEADS * GLA_DV)
AB_IN = sum(AB_SIZES)
AB_SPLIT_IDX = tuple(int(i) for i in np.cumsum(AB_SIZES)[:-1])
AB_MIX = MLA_HEADS * MLA_V + GLA_HEADS * GLA_DV
C_MIX = DIFF_HEADS * 2 * DIFF_HEAD_DIM

kernel_name = 'hybrid_mla_gla_diffattn_prefix_context_dit_step'


def rms_norm(x, g):
    xf = x.astype(jnp.float32)
    y = xf * lax.rsqrt(jnp.mean(xf * xf, axis=-1, keepdims=True) + NORM_EPS)
    return (y * g.astype(jnp.float32)).astype(x.dtype)


def axial_rope(n_tokens, rot_dim):
    rows = n_tokens // GRID_W
    row = jnp.repeat(jnp.arange(rows, dtype=jnp.float32), GRID_W)
    col = jnp.tile(jnp.arange(GRID_W, dtype=jnp.float32), rows)
    n_freq = rot_dim // 4
    inv = ROPE_BASE ** (-jnp.arange(n_freq, dtype=jnp.float32) / n_freq)
    ang = jnp.concatenate([row[:, None] * inv, col[:, None] * inv], axis=-1)
    return jnp.cos(ang), jnp.sin(ang)


def apply_rope(x, rope):
    cos, sin = rope
    n_tok, half = x.shape[1], x.shape[-1] // 2
    shape = (1, n_tok) + (1,) * (x.ndim - 3) + (half,)
    c, s = cos.reshape(shape), sin.reshape(shape)
    xf = x.astype(jnp.float32).reshape(x.shape[:-1] + (half, 2))
    xe, xo = xf[..., 0], xf[..., 1]
    out = jnp.stack([xe * c - xo * s, xe * s + xo * c], axis=-1).reshape(x.shape)
    return out.astype(x.dtype)


def sweep_query_blocks(fn, q):
    b, lq = q.shape[:2]
    nb = lq // Q_BLOCK
    qb = jnp.moveaxis(q.reshape((b, nb, Q_BLOCK) + q.shape[2:]), 1, 0)
    out = jnp.moveaxis(lax.map(fn, qb), 0, 1)
    return out.reshape((b, lq) + out.shape[3:])


def gla_chunked(q, k, v, g, s0):
    b, n_tok, h, _ = q.shape
    dv = v.shape[-1]
    n = n_tok // GLA_CHUNK

    def chunks(a):
        return a.astype(jnp.float32).reshape(b, n, GLA_CHUNK, h, a.shape[-1]).transpose(0, 3, 1, 2, 4)

    q, k, v, g = chunks(q), chunks(k), chunks(v), chunks(g)
    cum = jnp.cumsum(g, axis=3)
    cum_last = cum[:, :, :, -1:, :]
    q_in = q * jnp.exp(cum)
    k_in = k * jnp.exp(-cum)
    mask = jnp.tril(jnp.ones((GLA_CHUNK, GLA_CHUNK), dtype=bool))
    a = jnp.where(mask, jnp.einsum('bhncd,bhnsd->bhncs', q_in, k_in), 0.0)
    o_intra = jnp.einsum('bhncs,bhnsv->bhncv', a, v)
    u = jnp.einsum('bhncd,bhncv->bhndv', k * jnp.exp(cum_last - cum), v)
    decay = jnp.exp(cum_last[:, :, :, 0, :])

    def step(s, xs):
        dec, u_n = xs
        return dec[..., None] * s + u_n, s

    s_final, s_prev = lax.scan(step, s0.astype(jnp.float32),
                               (jnp.moveaxis(decay, 2, 0), jnp.moveaxis(u, 2, 0)))
    s_prev = jnp.moveaxis(s_prev, 0, 2)
    o = o_intra + jnp.einsum('bhncd,bhndv->bhncv', q_in, s_prev)
    return o.transpose(0, 2, 3, 1, 4).reshape(b, n_tok, h, dv), s_final


def mla_mix(q_lat, kv_lat, k_rope, ai, P, rope, ctx):
    b, n_tok, _ = q_lat.shape
    q = (rms_norm(q_lat, P['mla_g_q'][ai]) @ P['mla_w_uq'][ai]).reshape(b, n_tok, MLA_HEADS, MLA_NOPE + MLA_ROPE)
    q_nope, q_rope = q[..., :MLA_NOPE], q[..., MLA_NOPE:]
    ckv = rms_norm(kv_lat, P['mla_g_kv'][ai])
    if rope is not None:
        q_rope = apply_rope(q_rope, rope)
        k_rope = apply_rope(k_rope, rope)
    ckv_all, krope_all = ckv, k_rope
    if ctx is not None:
        ckv_all = jnp.concatenate([ctx[0].astype(ckv.dtype), ckv], axis=1)
        krope_all = jnp.concatenate([ctx[1].astype(k_rope.dtype), k_rope], axis=1)
    n_keys = ckv_all.shape[1]
    kv = (ckv_all @ P['mla_w_ukv'][ai]).reshape(b, n_keys, MLA_HEADS, MLA_NOPE + MLA_V)
    k_nope, v = kv[..., :MLA_NOPE], kv[..., MLA_NOPE:]
    scale = (MLA_NOPE + MLA_ROPE) ** -0.5

    def attend(qb):
        s = (jnp.einsum('bqhd,bkhd->bhqk', qb[..., :MLA_NOPE], k_nope)
             + jnp.einsum('bqhr,bkr->bhqk', qb[..., MLA_NOPE:], krope_all))
        p = jax.nn.softmax(s.astype(jnp.float32) * scale, axis=-1)
        return jnp.einsum('bhqk,bkhd->bqhd', p.astype(v.dtype), v)

    o = sweep_query_blocks(attend, jnp.concatenate([q_nope, q_rope], axis=-1))
    return o.reshape(b, n_tok, MLA_HEADS * MLA_V), (ckv, k_rope)


def gla_mix(gq, gk, gv, ggate, gout, ai, P, ctx):
    b, n_tok, _ = gq.shape
    q = gq.reshape(b, n_tok, GLA_HEADS, GLA_DK) * (GLA_DK ** -0.5)
    k = gk.reshape(b, n_tok, GLA_HEADS, GLA_DK)
    v = gv.reshape(b, n_tok, GLA_HEADS, GLA_DV)
    lr = ggate.reshape(b, n_tok, 2, GLA_GATE_RANK)
    pre = jnp.einsum('bldr,drk->bldk', lr, P['gla_w_gate_up'][ai]) + P['gla_b_gate'][ai]
    g = (jax.nn.log_sigmoid(pre.astype(jnp.float32)) / GLA_GATE_NORM).reshape(b, n_tok, 2, GLA_HEADS, GLA_DK)
    if ctx is None:
        s0f = jnp.zeros((b, GLA_HEADS, GLA_DK, GLA_DV), jnp.float32)
        s0b = s0f
    else:
        s0f, s0b = ctx
    o_f, s_f = gla_chunked(q, k, v, g[:, :, 0], s0f)
    o_b, s_b = gla_chunked(jnp.flip(q, 1), jnp.flip(k, 1), jnp.flip(v, 1), jnp.flip(g[:, :, 1], 1), s0b)
    o = (o_f + jnp.flip(o_b, 1)).astype(gq.dtype)
    o = rms_norm(o, P['gla_g_out'][ai]) * jax.nn.silu(gout.reshape(b, n_tok, GLA_HEADS, GLA_DV))
    return o.reshape(b, n_tok, GLA_HEADS * GLA_DV), (s_f, s_b)


def ab_mix(h, ai, P, rope, ctx):
    parts = jnp.split(h @ P['ab_w_in'][ai], AB_SPLIT_IDX, axis=-1)
    q_lat, kv_lat, k_rope, gq, gk, gv, ggate, gout = parts
    mla_out, (ckv, krope) = mla_mix(q_lat, kv_lat, k_rope, ai, P, rope,
                                    None if ctx is None else ctx[:2])
    gla_out, (s_f, s_b) = gla_mix(gq, gk, gv, ggate, gout, ai, P,
                                  None if ctx is None else ctx[2:])
    out = jnp.concatenate([mla_out, gla_out.astype(mla_out.dtype)], axis=-1) @ P['ab_w_out'][ai]
    return out, (ckv, krope, s_f, s_b)


def diff_lambda_init(layer):
    return 0.8 - 0.6 * math.exp(-0.3 * layer)


def diff_mix(h, li, ci, P, rope, ctx):
    b, n_tok, _ = h.shape
    q, k, v = jnp.split(h @ P['c_w_qkv'][ci], 3, axis=-1)
    q = q.reshape(b, n_tok, DIFF_HEADS, 2, DIFF_HEAD_DIM)
    k = k.reshape(b, n_tok, DIFF_HEADS, 2, DIFF_HEAD_DIM)
    v = v.reshape(b, n_tok, DIFF_HEADS, 2 * DIFF_HEAD_DIM)
    if rope is not None:
        q = apply_rope(q, rope)
        k = apply_rope(k, rope)
    k_all, v_all = k, v
    if ctx is not None:
        k_all = jnp.concatenate([ctx[0].astype(k.dtype), k], axis=1)
        v_all = jnp.concatenate([ctx[1].astype(v.dtype), v], axis=1)
    lp = P['diff_lambda'][ci].astype(jnp.float32)
    lam_init = diff_lambda_init(li)
    lam = jnp.exp(jnp.sum(lp[0] * lp[1])) - jnp.exp(jnp.sum(lp[2] * lp[3])) + lam_init
    scale = DIFF_HEAD_DIM ** -0.5

    def attend(qb):
        s = jnp.einsum('bqhjd,bkhjd->bhjqk', qb, k_all).astype(jnp.float32) * scale
        p = jax.nn.softmax(s, axis=-1)
        a = p[:, :, 0] - lam * p[:, :, 1]
        return jnp.einsum('bhqk,bkhe->bqhe', a.astype(v_all.dtype), v_all)

    o = sweep_query_blocks(attend, q)
    o = rms_norm(o, P['diff_g_out'][ci]) * (1.0 - lam_init)
    return o.reshape(b, n_tok, C_MIX) @ P['c_w_out'][ci], (k, v)


def trunk_layer(x, li, cond, P, rope, ctx):
    mod = (jax.nn.silu(cond) @ P['w_mod'][li] + P['b_mod'][li])[:, None, :]
    shift1, scale1, gate1, shift2, scale2, gate2 = jnp.split(mod, 6, axis=-1)
    g = P['g_norm'][li]
    h = rms_norm(x, g[0]) * (1.0 + scale1) + shift1
    if li % 2 == 0:
        out, new = ab_mix(h, li // 2, P, rope, ctx)
    else:
        out, new = diff_mix(h, li, li // 2, P, rope, ctx)
    x = x + gate1 * rms_norm(out, g[1])
    h = rms_norm(x, g[2]) * (1.0 + scale2) + shift2
    gate, up = jnp.split(h @ P['w_ffn_in'][li], 2, axis=-1)
    f = (jax.nn.silu(gate) * up) @ P['w_ffn_out'][li]
    x = x + gate2 * rms_norm(f, g[3])
    return x, new


def _normal(k, shape, scale):
    return jax.random.normal(k, shape, jnp.float32) * scale


def setup_inputs(seed: int = 0) -> dict:
    key = jax.random.key(seed)
    ks = jax.random.split(key, 32)
    d = D_MODEL
    return {
        'x_prompt': _normal(ks[0], (BATCH, SEQ, d), 1.0),
        'x_sample': _normal(ks[1], (DEC_BATCH, DEC_SEQ, d), 1.0),
        'cache_mla_ckv': _normal(ks[2], (DEC_BATCH, N_AB, PAST_LEN, MLA_KV_LORA), 1.0),
        'cache_mla_krope': _normal(ks[3], (DEC_BATCH, N_AB, PAST_LEN, MLA_ROPE), 1.0),
        'state_gla_fwd': _normal(ks[4], (DEC_BATCH, N_AB, GLA_HEADS, GLA_DK, GLA_DV), 1.0),
        'state_gla_bwd': _normal(ks[5], (DEC_BATCH, N_AB, GLA_HEADS, GLA_DK, GLA_DV), 1.0),
        'cache_diff_k': _normal(ks[6], (DEC_BATCH, N_C, PAST_LEN, DIFF_HEADS, 2, DIFF_HEAD_DIM), 1.0),
        'cache_diff_v': _normal(ks[7], (DEC_BATCH, N_C, PAST_LEN, DIFF_HEADS, 2 * DIFF_HEAD_DIM), 1.0),
        'c': _normal(ks[8], (DEC_BATCH, d), 1.0),
        'c_ctx': _normal(ks[9], (d,), 1.0),
        'w_mod': _normal(ks[10], (DEPTH, d, 6 * d), d ** -0.5),
        'b_mod': _normal(ks[11], (DEPTH, 6 * d), 0.02),
        'g_norm': 1.0 + _normal(ks[12], (DEPTH, 4, d), 0.02),
        'ab_w_in': _normal(ks[13], (N_AB, d, AB_IN), d ** -0.5),
        'mla_g_q': 1.0 + _normal(ks[14], (N_AB, MLA_Q_LORA), 0.02),
        'mla_g_kv': 1.0 + _normal(ks[15], (N_AB, MLA_KV_LORA), 0.02),
        'mla_w_uq': _normal(ks[16], (N_AB, MLA_Q_LORA, MLA_HEADS * (MLA_NOPE + MLA_ROPE)), MLA_Q_LORA ** -0.5),
        'mla_w_ukv': _normal(ks[17], (N_AB, MLA_KV_LORA, MLA_HEADS * (MLA_NOPE + MLA_V)), MLA_KV_LORA ** -0.5),
        'gla_w_gate_up': _normal(ks[18], (N_AB, 2, GLA_GATE_RANK, GLA_HEADS * GLA_DK), GLA_GATE_RANK ** -0.5),
        'gla_b_gate': _normal(ks[19], (N_AB, 2, GLA_HEADS * GLA_DK), 0.1),
        'gla_g_out': 1.0 + _normal(ks[20], (N_AB, GLA_DV), 0.02),
        'ab_w_out': _normal(ks[21], (N_AB, AB_MIX, d), AB_MIX ** -0.5),
        'c_w_qkv': _normal(ks[22], (N_C, d, 3 * C_MIX), d ** -0.5),
        'diff_lambda': _normal(ks[23], (N_C, 4, DIFF_HEAD_DIM), 0.1),
        'diff_g_out': 1.0 + _normal(ks[24], (N_C, 2 * DIFF_HEAD_DIM), 0.02),
        'c_w_out': _normal(ks[25], (N_C, C_MIX, d), C_MIX ** -0.5),
        'w_ffn_in': _normal(ks[26], (DEPTH, d, 2 * FFN_HIDDEN), d ** -0.5),
        'w_ffn_out': _normal(ks[27], (DEPTH, FFN_HIDDEN, d), FFN_HIDDEN ** -0.5),
    }


def reference(x_prompt, x_sample, cache_mla_ckv, cache_mla_krope, state_gla_fwd, state_gla_bwd,
              cache_diff_k, cache_diff_v, c, c_ctx, w_mod, b_mod, g_norm, ab_w_in, mla_g_q, mla_g_kv,
              mla_w_uq, mla_w_ukv, gla_w_gate_up, gla_b_gate, gla_g_out, ab_w_out, c_w_qkv, diff_lambda,
              diff_g_out, c_w_out, w_ffn_in, w_ffn_out):
    P = dict(w_mod=w_mod, b_mod=b_mod, g_norm=g_norm, ab_w_in=ab_w_in, mla_g_q=mla_g_q, mla_g_kv=mla_g_kv,
             mla_w_uq=mla_w_uq, mla_w_ukv=mla_w_ukv, gla_w_gate_up=gla_w_gate_up, gla_b_gate=gla_b_gate,
             gla_g_out=gla_g_out, ab_w_out=ab_w_out, c_w_qkv=c_w_qkv, diff_lambda=diff_lambda,
             diff_g_out=diff_g_out, c_w_out=c_w_out, w_ffn_in=w_ffn_in, w_ffn_out=w_ffn_out)

    y = x_prompt
    ab_states, c_states = [], []
    for li in range(DEPTH):
        y, new = trunk_layer(y, li, c_ctx[None, :], P, None, None)
        if li % 2 == 0:
            ab_states.append(new)
        else:
            c_states.append(new)
    y_prompt = y

    n_lat = x_sample.shape[1]
    rope_mla = axial_rope(n_lat, MLA_ROPE)
    rope_diff = axial_rope(n_lat, DIFF_HEAD_DIM)
    z = x_sample
    for li in range(DEPTH):
        i = li // 2
        if li % 2 == 0:
            ctx = (cache_mla_ckv[:, i], cache_mla_krope[:, i], state_gla_fwd[:, i], state_gla_bwd[:, i])
            z, _ = trunk_layer(z, li, c, P, rope_mla, ctx)
        else:
            ctx = (cache_diff_k[:, i], cache_diff_v[:, i])
            z, _ = trunk_layer(z, li, c, P, rope_diff, ctx)
    y_sample = z

    new_mla_ckv = jnp.stack([s[0] for s in ab_states], axis=1)
    new_mla_krope = jnp.stack([s[1] for s in ab_states], axis=1)
    new_gla_fwd = jnp.stack([s[2] for s in ab_states], axis=1)
    new_gla_bwd = jnp.stack([s[3] for s in ab_states], axis=1)
    new_diff_k = jnp.stack([s[0] for s in c_states], axis=1)
    new_diff_v = jnp.stack([s[1] for s in c_states], axis=1)
    return (y_prompt, y_sample, new_mla_ckv, new_mla_krope, new_gla_fwd, new_gla_bwd, new_diff_k, new_diff_v)
```

```python
import contextlib
import math
import numpy as np
import concourse.bass as bass
import concourse.mybir as mybir
from concourse.bass_utils import run_bass_kernel_spmd

F32 = mybir.dt.float32
BF16 = mybir.dt.bfloat16
ALU = mybir.AluOpType
AF = mybir.ActivationFunctionType
AX = mybir.AxisListType

ENGS = ("pe", "act", "dve", "pool", "sp")
D = 1024
DC = 8
TT = 512
EPS = 1e-6
FH = 2816
FC = 22
NIN = 3360


class Op:
    __slots__ = ("eng", "fn", "deps", "dma", "sig", "val")

    def __init__(self, eng, fn, dma):
        self.eng = eng
        self.fn = fn
        self.dma = dma
        self.deps = ()
        self.sig = False
        self.val = 0


class Prog:
    def __init__(self, nc):
        self.nc = nc
        self.ops = {e: [] for e in ENGS}
        self.lastw = {}
        self.readers = {}
        self.dma_cnt = {}
        self.dma_rr = {e: 0 for e in ENGS}
        self.dma_last = {}
        self.nsem = {"sp": 44, "pool": 44, "act": 6, "pe": 1, "dve": 1}

    def op(self, eng, fn, reads=(), writes=(), dma=None):
        if dma is not None:
            dma = (eng, self.dma_rr[eng] % self.nsem[eng])
            self.dma_rr[eng] += 1
        psr = [k for k in reads if isinstance(k, tuple) and k[0] == "ps"]
        if psr:
            reads = [k for k in reads if k not in psr]
            writes = list(writes) + psr
        o = Op(eng, fn, dma)
        deps = {}
        if dma is not None:
            prev = self.dma_last.get(dma)
            if prev is not None:
                deps[id(prev)] = prev
            self.dma_last[dma] = o
        for k in reads:
            w = self.lastw.get(k)
            if w is not None:
                deps[id(w)] = w
        for k in writes:
            w = self.lastw.get(k)
            if w is not None:
                deps[id(w)] = w
            for r in self.readers.get(k, ()):
                deps[id(r)] = r
        dl = []
        for d in deps.values():
            if d.eng == "pe" and eng == "pe" and d.dma is None and dma is None:
                continue
            dl.append(d)
        o.deps = dl
        for k in reads:
            lst = self.readers.setdefault(k, [])
            if dma is None:
                lst[:] = [r for r in lst if not (r.eng == eng and r.dma is None)]
            lst.append(o)
        for k in writes:
            self.lastw[k] = o
            self.readers[k] = []
        if dma is not None:
            self.dma_cnt[dma] = self.dma_cnt.get(dma, 0) + 16
            o.val = self.dma_cnt[dma]
        self.ops[eng].append(o)
        return o

    def barrier(self):
        lasts = []
        for e in ENGS:
            for o in reversed(self.ops[e]):
                if o.dma is None and o.fn is not None:
                    lasts.append(o)
                    break
        seen = {}
        for e in ENGS:
            for o in self.ops[e]:
                if o.dma is not None:
                    seen[o.dma] = o
        lasts.extend(seen.values())
        for e in ENGS:
            o = Op(e, None, None)
            o.deps = list(lasts)
            self.ops[e].append(o)
        self.lastw = {}
        self.readers = {}

    def check_deadlock(self):
        sem = {}
        pc = {e: 0 for e in ENGS}
        total = sum(len(v) for v in self.ops.values())
        done = 0
        while done < total:
            progressed = False
            for e in ENGS:
                while pc[e] < len(self.ops[e]):
                    o = self.ops[e][pc[e]]
                    ok = True
                    for d in o.deps:
                        k = ("d", d.dma) if d.dma is not None else ("e", d.eng)
                        assert d.val > 0, ("dep without signal value", e, pc[e])
                        if sem.get(k, 0) < d.val:
                            ok = False
                            break
                    if not ok:
                        break
                    if o.fn is not None:
                        if o.dma is not None:
                            k = ("d", o.dma)
                            sem[k] = sem.get(k, 0) + 16
                            assert sem[k] == o.val, ("dma sem order", k, sem[k], o.val)
                        elif o.sig:
                            k = ("e", e)
                            sem[k] = sem.get(k, 0) + 1
                            assert sem[k] == o.val
                    pc[e] += 1
                    done += 1
                    progressed = True
            if not progressed:
                raise RuntimeError("DEADLOCK in semaphore protocol at " + str(pc))
        print("[prog] ops per engine:", {e: len(self.ops[e]) for e in ENGS})

    def emit(self):
        nc = self.nc
        for e in ENGS:
            for o in self.ops[e]:
                for d in o.deps:
                    d.sig = True
        with contextlib.ExitStack() as st:
            esem = {e: st.enter_context(nc.semaphore("s_" + e)) for e in ENGS}
            dsem = {k: st.enter_context(nc.semaphore("d_%s%d" % k)) for k in self.dma_cnt}
            for e in ENGS:
                c = 0
                for o in self.ops[e]:
                    if o.dma is None and o.sig:
                        c += 1
                        o.val = c
            self.check_deadlock()
            block = st.enter_context(nc.Block())

            def body(ename):
                def f(eng):
                    waited = {}
                    for o in self.ops[ename]:
                        need = {}
                        for d in o.deps:
                            s = dsem[d.dma] if d.dma is not None else esem[d.eng]
                            key = id(s)
                            if key not in need or need[key][1] < d.val:
                                need[key] = (s, d.val)
                        for key, (s, v) in need.items():
                            if waited.get(key, 0) >= v:
                                continue
                            eng.wait_ge(s, v)
                            waited[key] = v
                        if o.fn is None:
                            continue
                        ins = o.fn(eng)
                        if o.dma is not None:
                            ins.then_inc(dsem[o.dma], 16)
                        elif o.sig:
                            ins.then_inc(esem[ename], 1)

                return f

            block.tensor(body("pe"))
            block.scalar(body("act"))
            block.vector(body("dve"))
            block.gpsimd(body("pool"))
            block.sync(body("sp"))


class Arena:
    def __init__(self, nc, nbytes=212000):
        self.n4 = nbytes // 4
        self.t = nc.alloc_sbuf_tensor("arena", [128, self.n4], F32)
        self.off = 0

    def alloc(self, shape, dt):
        esz = 4 if dt == F32 else 2
        n = int(np.prod(shape[1:]))
        nb = (n * esz + 31) // 32 * 32
        assert self.off + nb <= self.n4 * 4, ("SBUF arena overflow", self.off, nb)
        o4 = self.off // 4
        v = self.t[:, o4:o4 + nb // 4]
        if dt != F32:
            v = v.bitcast(dt)
        v = v[:, 0:n]
        if len(shape) > 2:
            names = " ".join(f"d{i}" for i in range(1, len(shape)))
            kw = {f"d{i}": shape[i] for i in range(1, len(shape))}
            v = v.rearrange(f"p ({names}) -> p {names}", **kw)
        if shape[0] < 128:
            v = v[0:shape[0]]
        self.off += nb
        return v


def bc_mid(ap, n):
    a = ap.ap
    return bass.AP(ap.tensor, ap.offset, [list(a[0]), [0, n]] + [list(x) for x in a[1:]])


def bc_part(ap1d, nparts):
    return bass.AP(ap1d.tensor, ap1d.offset, [[0, nparts]] + [list(x) for x in ap1d.ap])


class Cfg:
    def __init__(self, LS=4096, CTX=512, LP=256, NPS=2):
        self.LS, self.CTX, self.LP, self.NPS = LS, CTX, LP, NPS


class Grp:
    pass


class StopBuild(Exception):
    pass


class K:
    def __init__(self, cfg, stop_after=None):
        self.cfg = cfg
        self.stop_after = stop_after
        nc = self.nc = bass.Bass("TRN2", target_bir_lowering=False)
        self.P = Prog(nc)
        self.AR = Arena(nc)
        self.PSP = [nc.alloc_psum_tensor(f"pp{i}", [128, 1024], F32)[:] for i in range(4)]
        self.PS = [self.PSP[i // 2][:, (i % 2) * 512:(i % 2 + 1) * 512] for i in range(8)]
        self.pi = 0
        self.pool_banks = list(range(8))
        self.uid = 0
        self.io()
        self.groups()

    def din(self, name, shape, dt=F32):
        return self.nc.dram_tensor(name, list(shape), dt, kind="ExternalInput").ap()

    def dout(self, name, shape):
        return self.nc.dram_tensor(name, list(shape), F32, kind="ExternalOutput").ap()

    def dscr(self, name, shape, dt):
        return self.nc.dram_tensor(name, list(shape), dt, kind="Internal").ap()

    def io(self):
        c = self.cfg
        NPT = c.NPS * c.LP
        self.x_s = self.din("x_s", [c.LS, D])
        self.x_p = self.din("x_p", [NPT, D])
        self.ckv_c = self.din("ckv_c", [c.CTX, 256])
        self.kr_c = self.din("kr_c", [c.CTX, 32])
        self.sf_in = self.din("sf_in", [4, 128, 128])
        self.sb_in = self.din("sb_in", [4, 128, 128])
        self.dk_c = self.din("dk_c", [c.CTX, D])
        self.dv_c = self.din("dv_c", [c.CTX, D])
        self.smallv = self.din("smallv", [84, 128])
        self.w_mod = self.din("w_mod", [2, D, 6144])
        self.b_mod = self.din("b_mod", [2, 6144])
        self.w_in = self.din("w_in", [D, NIN])
        self.g_kv = self.din("g_kv", [256])
        self.w_uq = self.din("w_uq", [384, 1536])
        self.w_ukv = self.din("w_ukv", [256, 1024])
        self.w_up = self.din("w_up", [2, 16, 512])
        self.b_up = self.din("b_up", [2, 512])
        self.g_gla = self.din("g_gla", [128])
        self.w_o0 = self.din("w_o0", [D, D])
        self.w_qkv = self.din("w_qkv", [D, 5120])
        self.dlam = self.din("dlam", [256])
        self.w_o1 = self.din("w_o1", [D, D])
        self.w_f1 = self.din("w_f1", [2, D, 2 * FH])
        self.w_f2 = self.din("w_f2", [2, FH, D])
        self.ropeM = self.din("ropeM", [2, 32, c.LS])
        self.ropeD = self.din("ropeD", [2, 128, c.LS])
        self.perm_in = self.din("perm", [128, 128])
        self.y_p = self.dout("y_p", [NPT, D])
        self.y_s = self.dout("y_s", [c.LS, D])
        self.o_ckv = self.dout("o_ckv", [NPT, 256])
        self.o_kr = self.dout("o_kr", [NPT, 32])
        self.o_sf = self.dout("o_sf", [c.NPS, 4, 128, 128])
        self.o_sb = self.dout("o_sb", [c.NPS, 4, 128, 128])
        self.o_dk = self.dout("o_dk", [NPT, D])
        self.o_dv = self.dout("o_dv", [NPT, D])
        self.wb_in = self.dscr("wb_in", [128, 8, NIN], BF16)
        self.wb_uq = self.dscr("wb_uq", [128, 3, 1536], BF16)
        self.wb_ukv = self.dscr("wb_ukv", [128, 2, 1024], BF16)
        self.wb_o0 = self.dscr("wb_o0", [128, 8, D], BF16)
        self.wb_qkv = self.dscr("wb_qkv", [128, 8, 5120], BF16)
        self.wb_o1 = self.dscr("wb_o1", [128, 8, D], BF16)
        self.wb_f1 = [self.dscr(f"wb_f1_{l}", [128, 8, 2 * FH], BF16) for l in range(2)]
        self.wb_f2 = [self.dscr(f"wb_f2_{l}", [128, FC, D], BF16) for l in range(2)]

    def groups(self):
        c = self.cfg
        gs = Grp()
        gs.name, gs.gi, gs.ntok, gs.ctx, gs.rope, gs.prompt = "s", 0, c.LS, c.CTX, True, False
        gs.x_in, gs.y_out = self.x_s, self.y_s
        gs.seqs = [(0, c.LS)]
        gs.att = [(q0, TT, 0, c.CTX + c.LS) for q0 in range(0, c.LS, TT)]
        gp = Grp()
        gp.name, gp.gi, gp.ntok, gp.ctx, gp.rope, gp.prompt = "p", 1, c.NPS * c.LP, 0, False, True
        gp.x_in, gp.y_out = self.x_p, self.y_p
        gp.seqs = [(i * c.LP, c.LP) for i in range(c.NPS)]
        gp.att = [(i * c.LP, c.LP, i * c.LP, c.LP) for i in range(c.NPS)]
        for g in (gs, gp):
            assert g.ntok % TT == 0
            g.nt = g.ntok // TT
            g.nk = g.ctx + g.ntok
            n = g.name
            g.xT = self.dscr(f"xT_{n}", [g.nt, 128, DC, TT], F32)
            g.mixT = self.dscr(f"mixT_{n}", [g.nt, 128, DC, TT], BF16)
            g.Kf = self.dscr(f"Kf_{n}", [8, 128, g.nk], BF16)
            g.Qf = self.dscr(f"Qf_{n}", [8, 128, g.ntok], BF16)
            g.Vf = self.dscr(f"Vf_{n}", [8, 128, g.nk // 128, 128], BF16)
            g.gqT = self.dscr(f"gqT_{n}", [4, 128, g.ntok], BF16)
            g.gkT = self.dscr(f"gkT_{n}", [4, 128, g.ntok], BF16)
            g.gk_tok = self.dscr(f"gkt_{n}", [g.ntok // 128, 128, 512], BF16)
            g.gv_tok = self.dscr(f"gvt_{n}", [g.ntok // 128, 128, 512], BF16)
            g.go_tok = self.dscr(f"got_{n}", [g.ntok // 128, 128, 512], BF16)
            g.lr = self.dscr(f"lr_{n}", [64, g.ntok], F32)
            g.of = self.dscr(f"of_{n}", [g.ntok // 128, 128, 512], F32)
        self.G = [gs, gp]

    def bank(self):
        i = self.pool_banks[self.pi % len(self.pool_banks)]
        self.pi += 1
        return i

    def mm(self, out, lhsT, rhs, start=True, stop=True, r=(), w=()):
        self.P.op("pe", lambda e: e.matmul(out, lhsT, rhs, start=start, stop=stop), reads=r, writes=w)

    def tr(self, out, in_, ident, r=(), w=()):
        self.P.op("pe", lambda e: e.transpose(out, in_, ident), reads=r, writes=w)

    def act(self, out, in_, func, r=(), w=(), bias=None, scale=None, accum=None):
        kw = {}
        if bias is not None:
            kw["bias"] = bias
        if scale is not None:
            kw["scale"] = scale
        if accum is not None:
            kw["accum_out"] = accum
        self.P.op("act", lambda e: e.activation(out, in_, func, **kw), reads=r, writes=w)

    def tt(self, eng, out, in0, in1, op, r=(), w=()):
        self.P.op(eng, lambda e: e.tensor_tensor(out, in0, in1, op), reads=r, writes=w)

    def ts(self, eng, out, in0, s1, s2, op0, op1=None, r=(), w=()):
        if op1 is None:
            self.P.op(eng, lambda e: e.tensor_scalar(out, in0, s1, None, op0), reads=r, writes=w)
        else:
            self.P.op(eng, lambda e: e.tensor_scalar(out, in0, s1, s2, op0, op1), reads=r, writes=w)

    def stt(self, eng, out, in0, scalar, in1, op0, op1, r=(), w=()):
        self.P.op(eng, lambda e: e.scalar_tensor_tensor(out, in0, scalar, in1, op0, op1), reads=r, writes=w)

    def cp(self, eng, out, in_, r=(), w=()):
        if eng == "act":
            self.P.op("act", lambda e: e.activation(out, in_, AF.Copy), reads=r, writes=w)
        else:
            self.P.op(eng, lambda e: e.tensor_copy(out, in_), reads=r, writes=w)

    def recip(self, out, in_, r=(), w=()):
        self.P.op("dve", lambda e: e.reciprocal(out, in_), reads=r, writes=w)

    def memset(self, eng, ap, val, w=()):
        self.P.op(eng, lambda e: e.memset(ap, val), writes=w)

    def dma(self, q, out, in_, r=(), w=(), sem=None, accum=False):
        assert sem is not None
        if accum:
            self.P.op(q, lambda e: e.dma_start(out=out, in_=in_, accum_op=ALU.add), reads=r, writes=w, dma=sem)
        else:
            self.P.op(q, lambda e: e.dma_start(out=out, in_=in_), reads=r, writes=w, dma=sem)

    def rstd_from(self, out_sb, ss_ap, n, r, w):
        self.act(out_sb, ss_ap, AF.Ln, r=r, w=w, scale=1.0 / n, bias=self.eps_t[0:out_sb.shape[0], :])
        self.act(out_sb, out_sb, AF.Exp, r=w, w=w, scale=-0.5)

    def ck(self, name):
        if self.stop_after == name:
            raise StopBuild()

    def build(self):
        try:
            return self.build_()
        except StopBuild:
            return self.finish()

    def build_(self):
        self.setup()
        if self.stop_after in ("setup", "s0", "s1", "s2", "s3", "s4", "s5"):
            return self.finish()
        for g in self.G:
            self.phaseA0(g)
        if self.stop_after == "A0":
            return self.finish()
        for g in self.G:
            self.attn_phase(g, layer=0)
        if self.stop_after == "B0":
            return self.finish()
        for g in self.G:
            self.gla_phase(g)
        if self.stop_after == "C0":
            return self.finish()
        self.phaseD(0)
        if self.stop_after == "D0":
            return self.finish()
        for g in self.G:
            self.phaseA1(g)
        if self.stop_after == "A1":
            return self.finish()
        for g in self.G:
            self.attn_phase(g, layer=1)
        self.phaseD(1)
        return self.finish()

    def finish(self):
        self.P.barrier()
        self.P.emit()
        return self.nc

    def phase_begin(self):
        self.P.barrier()
        self.AR.off = self.persist_off
        self.pool_banks = list(range(8))

    def setup(self):
        A = self.AR
        P = self.P
        self.ident_f = A.alloc([128, 128], F32)
        self.ident_b = A.alloc([128, 128], BF16)
        self.ones_b = A.alloc([128, 128], BF16)
        self.ones_f = A.alloc([128, 128], F32)
        self.triLE = A.alloc([128, 128], F32)
        self.triGE = A.alloc([128, 128], F32)
        self.triLT = A.alloc([128, 128], F32)
        self.triGT = A.alloc([128, 128], F32)
        self.eps_t = A.alloc([128, 1], F32)
        self.pp = A.alloc([128, 84], F32)
        self.scd = A.alloc([128, 8, 2], F32)
        self.modpp = A.alloc([128, 2, 48, 2], F32)
        self.sc = A.alloc([128, 2, 2, 4, 8], F32)
        self.g_kv_bc = A.alloc([128, 256], F32)
        self.g_gla_bc = A.alloc([128, 128], F32)
        self.lam_t = A.alloc([128, 8], F32)
        self.wup = A.alloc([64, 512], F32)
        self.permB = A.alloc([128, 128], BF16)
        self.persist_off = A.off

        self.memset("pool", self.ident_f, 0.0, w=["ident_f"])
        P.op("pool", lambda e: e.affine_select(self.ident_f, self.ident_f, [[-1, 128]], ALU.not_equal, 1.0, base=0, channel_multiplier=1), reads=["ident_f"], writes=["ident_f"])
        self.cp("pool", self.ident_b, self.ident_f, r=["ident_f"], w=["ident_b"])
        self.memset("pool", self.ones_b, 1.0, w=["ones_b"])
        self.memset("pool", self.ones_f, 1.0, w=["ones_f"])
        self.memset("pool", self.eps_t, EPS, w=["eps"])
        for nm, t, pat, cm, cmp_ in (("triLE", self.triLE, 1, -1, ALU.is_ge), ("triGE", self.triGE, -1, 1, ALU.is_ge),
                                     ("triLT", self.triLT, 1, -1, ALU.is_gt), ("triGT", self.triGT, -1, 1, ALU.is_gt)):
            self.memset("pool", t, 1.0, w=[nm])
            P.op("pool", (lambda t, pat, cm, cmp_: (lambda e: e.affine_select(t, t, [[pat, 128]], cmp_, 0.0, base=0, channel_multiplier=cm)))(t, pat, cm, cmp_), reads=[nm], writes=[nm])

        if self.stop_after == "s0":
            return
        def cast(dst, src, kc, n, key):
            sv = src.rearrange("(kc p) n -> p kc n", p=128)
            step = 1408 if n > 1408 else n
            for i in range(0, n, step):
                j = min(n, i + step)
                self.dma("pool", dst[:, :, i:j], sv[:, :, i:j], w=[key + str(i)], sem="cast")
        cast(self.wb_in, self.w_in, 8, NIN, "wb_in")
        cast(self.wb_uq, self.w_uq, 3, 1536, "wb_uq")
        cast(self.wb_ukv, self.w_ukv, 2, 1024, "wb_ukv")
        cast(self.wb_o0, self.w_o0, 8, D, "wb_o0")
        self.cast = cast

        if self.stop_after == "s1":
            return
        pstg = A.alloc([128, 128], F32)
        self.dma("sp", pstg, self.perm_in, w=["pstg"], sem="x")
        self.cp("dve", self.permB, pstg, r=["pstg"], w=["permB"])
        stg = A.alloc([128, 128], F32)
        self.dma("sp", stg[0:84, :], self.smallv, w=["stg"], sem="stg")
        b = self.bank()
        self.tr(self.PS[b][:, 0:84], stg[0:84, :], self.ident_f[0:84, 0:84], r=["stg", "ident_f"], w=[("ps", b)])
        self.cp("dve", self.pp, self.PS[b][:, 0:84], r=[("ps", b)], w=["pp"])
        for g in range(2):
            self.act(self.scd[:, :, g], self.pp[:, 68 + g * 8:76 + g * 8], AF.Silu, r=["pp"], w=[("scd", g)])
        self.dma("sp", self.g_kv_bc, bc_part(self.g_kv, 128), w=["g_kv_bc"], sem="c1")
        self.dma("sp", self.g_gla_bc, bc_part(self.g_gla, 128), w=["g_gla_bc"], sem="c2")
        self.memset("pool", self.wup, 0.0, w=["wup"])
        self.dma("sp", self.wup[0:16, :], self.w_up[0], r=["wup"], w=["wup0"], sem="c3")
        self.dma("sp", self.wup[16:17, :], self.b_up[0:1, :], r=["wup"], w=["wup1"], sem="c4")
        self.dma("sp", self.wup[32:48, :], self.w_up[1], r=["wup"], w=["wup2"], sem="c5")
        self.dma("sp", self.wup[48:49, :], self.b_up[1:2, :], r=["wup"], w=["wup3"], sem="c6")

        if self.stop_after == "s2":
            return
        dl = A.alloc([128, 256], F32)
        dl2 = A.alloc([128, 128], F32)
        self.dma("sp", dl, bc_part(self.dlam, 128), w=["dl"], sem="c7")
        dlv = dl.rearrange("p (a b d) -> p a b d", a=2, b=2)
        self.tt("dve", dl2.rearrange("p (a d) -> p a d", a=2), dlv[:, :, 0, :], dlv[:, :, 1, :], ALU.mult, r=["dl"], w=["dl2"])
        lam2 = A.alloc([128, 2], F32)
        P.op("dve", lambda e: e.tensor_reduce(lam2, dl2.rearrange("p (a d) -> p a d", a=2), AX.X, ALU.add), reads=["dl2"], writes=["lam2"])
        self.act(lam2, lam2, AF.Exp, r=["lam2"], w=["lam2"])
        lam_init = 0.8 - 0.6 * math.exp(-0.3 * 1)
        self.stt("dve", self.lam_t[:, 0:1], lam2[:, 1:2], -lam_init, lam2[:, 0:1], ALU.add, ALU.subtract, r=["lam2"], w=["lam_t0"])
        self.ts("dve", self.lam_t[:, 1:2], self.pp[:, 67:68], 1.0 - lam_init, None, ALU.mult, r=["pp"], w=["lam_t1"])

        if self.stop_after == "s3":
            return
        mod_sb = A.alloc([2, 6144], F32)
        bmod_sb = A.alloc([2, 6144], F32)
        NWS = 5
        wslab = [A.alloc([128, 8, 512], F32) for _ in range(NWS)]
        si = 0
        for l in range(2):
            self.dma("sp", bmod_sb, bc_part(self.b_mod[l], 2), r=[], w=["bmod"], sem="bmod")
            wv = self.w_mod[l].rearrange("(kc p) n -> p kc n", p=128)
            for n in range(12):
                s = si % NWS
                si += 1
                self.dma("sp", wslab[s], wv[:, :, n * 512:(n + 1) * 512], w=[("wslab", s)], sem=f"wslab{s}")
                b = self.bank()
                for kc in range(8):
                    self.mm(self.PS[b][0:2, :], self.scd[:, kc, :], wslab[s][:, kc, :], start=(kc == 0), stop=(kc == 7),
                            r=[("wslab", s), ("scd", 0), ("scd", 1)], w=[("ps", b)])
                self.tt("dve", mod_sb[:, n * 512:(n + 1) * 512], self.PS[b][0:2, :], bmod_sb[:, n * 512:(n + 1) * 512], ALU.add,
                        r=[("ps", b), "bmod"], w=[("mod_sb", n)])
            if self.stop_after == "s4":
                continue
            b = self.bank()
            for j in range(48):
                self.tr(self.PS[b][:, 2 * j:2 * j + 2], mod_sb[:, j * 128:(j + 1) * 128], self.ident_f[0:2, 0:2],
                        r=[("mod_sb", j // 4), "ident_f"], w=[("ps", b)])
            self.cp("dve", self.modpp[:, l].rearrange("p j g -> p (j g)"), self.PS[b][:, 0:96], r=[("ps", b)], w=[("modpp", l)])
            if self.stop_after == "s5":
                continue
            for g in range(2):
                mp = self.modpp[:, l, :, g]
                gn = lambda f: self.pp[:, (l * 4 + f) * 8:(l * 4 + f) * 8 + 8]
                self.stt("dve", self.sc[:, l, g, 0, :], mp[:, 8:16], 1.0, gn(0), ALU.add, ALU.mult, r=[("modpp", l), "pp"], w=[("sc", l, g, 0)])
                self.tt("dve", self.sc[:, l, g, 1, :], mp[:, 16:24], gn(1), ALU.mult, r=[("modpp", l), "pp"], w=[("sc", l, g, 1)])
                self.stt("dve", self.sc[:, l, g, 2, :], mp[:, 32:40], 1.0, gn(2), ALU.add, ALU.mult, r=[("modpp", l), "pp"], w=[("sc", l, g, 2)])
                self.tt("dve", self.sc[:, l, g, 3, :], mp[:, 40:48], gn(3), ALU.mult, r=[("modpp", l), "pp"], w=[("sc", l, g, 3)])

    def sc_a(self, l, g, kind, c):
        return self.sc[:, l, g, kind, c:c + 1]

    def sc_shift(self, l, g, which, c):
        return self.modpp[:, l, which * 8 + c, g:g + 1]

    def load_xT_from_input(self, g, t, xT):
        A = self.AR
        xt = self.x_tok
        self.dma("sp", xt, g.x_in[t * TT:(t + 1) * TT, :].rearrange("(s p) d -> p s d", p=128), w=["x_tok"], sem="x_tok")
        for c in range(DC):
            b = self.bank()
            for s in range(4):
                self.tr(self.PS[b][:, s * 128:(s + 1) * 128], xt[:, s, c * 128:(c + 1) * 128], self.ident_f, r=["x_tok", "ident_f"], w=[("ps", b)])
            self.cp("act" if c % 2 else "dve", xT[:, c, :], self.PS[b], r=[("ps", b)], w=[("xT", c)])
        self.dma("pool", g.xT[t], xT, r=[("xT", c) for c in range(DC)], w=[("xTs", g.name, t)], sem="xT_st")

    def norm_mod(self, xT, hT, l, g, which, sq, rstd, tmp, xkeys, hkeys=None):
        if hkeys is None:
            hkeys = [("hT", c) for c in range(DC)]
        kind = 0 if which == 0 else 2
        shw = 0 if which == 0 else 3
        for c in range(DC):
            self.act(sq[:, c, :], xT[:, c, :], AF.Square, r=[xkeys[c]], w=[("sq", c)])
        b = self.bank()
        for c in range(DC):
            self.mm(self.PS[b], self.ones_b, sq[:, c, :], start=(c == 0), stop=(c == DC - 1), r=[("sq", c), "ones_b"], w=[("ps", b)])
        self.rstd_from(rstd, self.PS[b], D, r=[("ps", b), "eps"], w=["rstd"])
        for c in range(DC):
            self.stt("dve", tmp[:, c, :], xT[:, c, :], self.sc_a(l, g.gi, kind, c), rstd, ALU.mult, ALU.mult,
                     r=[xkeys[c], "rstd", ("sc", l, g.gi, kind)], w=[("tmp", c)])
            self.act(hT[:, c, :], tmp[:, c, :], AF.Identity, r=[("tmp", c), ("modpp", l)], w=[hkeys[c]],
                     bias=self.sc_shift(l, g.gi, shw, c))

    def phaseA0(self, g):
        self.phase_begin()
        A = self.AR
        l = 0
        if g.gi == 0:
            self.cast(self.wb_f1[0], self.w_f1[0], 8, 2 * FH, "wb_f10")
            self.cast(self.wb_f2[0], self.w_f2[0], FC, D, "wb_f20")
        W = A.alloc([128, 8, NIN], BF16)
        Wuq = A.alloc([128, 3, 1536], BF16)
        Wukv = A.alloc([128, 2, 1024], BF16)
        self.dma("sp", W, self.wb_in, r=["wb_in" + str(i) for i in range(0, NIN, 1408)], w=["W"], sem="W")
        self.dma("sp", Wuq, self.wb_uq, r=["wb_uq0", "wb_uq1408"], w=["Wuq"], sem="Wuq")
        self.dma("sp", Wukv, self.wb_ukv, r=["wb_ukv0"], w=["Wukv"], sem="Wukv")
        self.x_tok = A.alloc([128, 4, D], F32)
        xT = A.alloc([128, DC, TT], F32)
        hTs = [A.alloc([128, DC, TT], BF16) for _ in range(2)]
        sq = A.alloc([128, DC, TT], BF16)
        tmp = A.alloc([128, DC, TT], F32)
        rstd = A.alloc([128, TT], F32)
        qlat = A.alloc([128, 3, TT], F32)
        qn = A.alloc([128, 3, TT], BF16)
        QT = A.alloc([96, 8, TT], BF16)
        rtmp = A.alloc([96, 2, TT], F32)
        krT = A.alloc([32, TT], BF16)
        lrT = A.alloc([64, TT], F32)
        gqs = A.alloc([128, 4, TT], BF16)
        gks = A.alloc([128, 4, TT], BF16)
        knT = A.alloc([128, 4, TT], BF16)
        ropeC = A.alloc([96, TT], F32)
        ropeS = A.alloc([96, TT], F32)
        kvns = [A.alloc([128, 288], F32) for _ in range(4)]
        sss = [A.alloc([128, 2], F32) for _ in range(4)]
        junk = A.alloc([128, 256], F32)
        ckvT = A.alloc([128, 2, TT], BF16)
        tokb = [A.alloc([128, 3, 512], BF16) for _ in range(2)]
        vst = [A.alloc([128, 8, 64], BF16) for _ in range(2)]
        self.memset("pool", lrT, 1.0, w=["lrT"])
        self.ck("a0a")
        ctx_x = None
        if g.ctx:
            ctx_x = A.alloc([128, 288], F32)

        def kv_path(ckvT_keys, k0, nkeys):
            for hp in range(4):
                b = self.bank()
                for kc in range(2):
                    self.mm(self.PS[b][:, 0:nkeys], Wukv[:, kc, hp * 128:(hp + 1) * 128], ckvT[:, kc, 0:nkeys], start=(kc == 0), stop=(kc == 1),
                            r=["Wukv"] + ckvT_keys, w=[("ps", b)])
                self.cp("act", knT[:, hp, 0:nkeys], self.PS[b][:, 0:nkeys], r=[("ps", b)], w=[("knT", hp)])
                for hh in range(2):
                    h = hp * 2 + hh
                    self.dma("pool", g.Kf[h, 0:64, k0:k0 + nkeys], knT[hh * 64:(hh + 1) * 64, hp, 0:nkeys], r=[("knT", hp)], w=[("Kf", h, k0, 0)], sem="knT_st")
            for s in range(nkeys // 128):
                b = self.bank()
                for kc in range(2):
                    self.mm(self.PS[b], ckvT[:, kc, s * 128:(s + 1) * 128], Wukv[:, kc, 512:1024], start=(kc == 0), stop=(kc == 1),
                            r=["Wukv"] + ckvT_keys, w=[("ps", b)])
                vs = vst[s % 2]
                self.cp("dve", vs.rearrange("p h e -> p (h e)"), self.PS[b], r=[("ps", b)], w=[("vst", s % 2)])
                kt = (k0 + s * 128) // 128
                self.dma("pool", g.Vf[:, :, kt, 0:64].rearrange("h p e -> p h e"), vs, r=[("vst", s % 2)], w=[("Vf", kt)], sem=f"vst{s % 2}")

        if g.ctx:
            for k0 in range(0, g.ctx, TT):
                nkeys = min(TT, g.ctx - k0)
                for s in range(nkeys // 128):
                    r0 = k0 + s * 128
                    self.dma("sp", ctx_x[:, 0:256], self.ckv_c[r0:r0 + 128, :], w=["ctx_x"], sem="ctx_x")
                    self.dma("sp", ctx_x[:, 256:288], self.kr_c[r0:r0 + 128, :], w=["ctx_x2"], sem="ctx_x2")
                    self.ck("a0b1")
                    b = self.bank()
                    for kc in range(2):
                        self.tr(self.PS[b][:, kc * 128:(kc + 1) * 128], ctx_x[:, kc * 128:(kc + 1) * 128], self.ident_f, r=["ctx_x", "ident_f"], w=[("ps", b)])
                    self.ck("a0b2")
                    self.tr(self.PS[b][0:32, 256:384], ctx_x[:, 256:288], self.ident_f, r=["ctx_x2", "ident_f"], w=[("ps", b)])
                    self.ck("a0b3")
                    self.cp("dve", ckvT[:, :, s * 128:(s + 1) * 128], self.PS[b][:, 0:256].rearrange("p (k t) -> p k t", k=2), r=[("ps", b)], w=[("ckvT", s)])
                    self.ck("a0b4")
                    self.cp("act", krT[:, s * 128:(s + 1) * 128], self.PS[b][0:32, 256:384], r=[("ps", b)], w=[("krT", s)])
                ns = nkeys // 128
                self.ck("a0b")
                for h in range(8):
                    self.dma("pool", g.Kf[h, 64:96, k0:k0 + nkeys], krT[:, 0:nkeys], r=[("krT", s) for s in range(ns)], w=[("Kf", h, k0, 1)], sem="krT_st")
                self.ck("a0c")
                kv_path([("ckvT", s) for s in range(ns)], k0, nkeys)

        def head(t):
            xkeys = [("xT", c) for c in range(DC)]
            self.load_xT_from_input(g, t, xT)
            self.norm_mod(xT, hTs[t % 2], l, g, 0, sq, rstd, tmp, xkeys, hkeys=[("hT", t % 2, c) for c in range(DC)])
            if g.rope:
                self.dma("sp", ropeC[64:96, :], self.ropeM[0, :, t * TT:(t + 1) * TT], w=["ropeC"], sem="ropeC")
                self.dma("sp", ropeS[64:96, :], self.ropeM[1, :, t * TT:(t + 1) * TT], w=["ropeS"], sem="ropeS")
                self.dma("sp", ropeC[0:32, :], self.ropeM[0, :, t * TT:(t + 1) * TT], w=["ropeCk"], sem="ropeCk")
                self.dma("sp", ropeS[0:32, :], self.ropeM[1, :, t * TT:(t + 1) * TT], w=["ropeSk"], sem="ropeSk")

        head(0)
        for t in range(g.nt):
            hT = hTs[t % 2]
            hk = [("hT", t % 2, c) for c in range(DC)]

            def projB(col0, m, ps_ap, b):
                for kc in range(DC):
                    self.mm(ps_ap, W[:, kc, col0:col0 + m], hT[:, kc, :], start=(kc == 0), stop=(kc == DC - 1), r=["W", hk[kc]], w=[("ps", b)])

            for c in range(3):
                b = self.bank()
                projB(c * 128, 128, self.PS[b], b)
                self.cp("act", qlat[:, c, :], self.PS[b], r=[("ps", b)], w=[("qlat", c)])
                self.tt("pool", sq[:, c, :], qlat[:, c, :], qlat[:, c, :], ALU.mult, r=[("qlat", c)], w=[("sq", c)])
            b = self.bank()
            for c in range(3):
                self.mm(self.PS[b], self.ones_b, sq[:, c, :], start=(c == 0), stop=(c == 2), r=[("sq", c), "ones_b"], w=[("ps", b)])
            self.rstd_from(rstd, self.PS[b], 384, r=[("ps", b), "eps"], w=["rstd"])
            for c in range(3):
                self.stt("dve", qn[:, c, :], qlat[:, c, :], self.pp[:, 64 + c:65 + c], rstd, ALU.mult, ALU.mult, r=[("qlat", c), "rstd", "pp"], w=[("qn", c)])
            self.ck("a3")
            for h in range(8):
                b = self.bank()
                for kc in range(3):
                    self.mm(self.PS[b][0:96, :], Wuq[:, kc, h * 96:(h + 1) * 96], qn[:, kc, :], start=(kc == 0), stop=(kc == 2), r=["Wuq", ("qn", kc)], w=[("ps", b)])
                if g.rope:
                    b2 = self.bank()
                    for kc in range(3):
                        self.mm(self.PS[b2][0:96, :], Wuq[:, kc, 768 + h * 96:768 + (h + 1) * 96], qn[:, kc, :], start=(kc == 0), stop=(kc == 2), r=["Wuq", ("qn", kc)], w=[("ps", b2)])
                    self.cp("act", QT[0:64, h, :], self.PS[b][0:64, :], r=[("ps", b)], w=[("QT", h, 0)])
                    self.tt("dve", rtmp[64:96, 0, :], self.PS[b][64:96, :], ropeC[64:96, :], ALU.mult, r=[("ps", b), "ropeC"], w=[("rtmp", 0)])
                    self.tt("dve", rtmp[64:96, 1, :], self.PS[b2][64:96, :], ropeS[64:96, :], ALU.mult, r=[("ps", b2), "ropeS"], w=[("rtmp", 1)])
                    self.tt("pool", QT[64:96, h, :], rtmp[64:96, 0, :], rtmp[64:96, 1, :], ALU.add, r=[("rtmp", 0), ("rtmp", 1)], w=[("QT", h, 1)])
                else:
                    self.cp("act", QT[:, h, :], self.PS[b][0:96, :], r=[("ps", b)], w=[("QT", h, 0), ("QT", h, 1)])
            self.dma("pool", g.Qf[:, 0:96, t * TT:(t + 1) * TT].rearrange("h p t -> p h t"), QT,
                     r=[("QT", h, i) for h in range(8) for i in range(2)], w=[("Qf", t)], sem="QT_st")
            self.ck("a4")
            b = self.bank()
            projB(384, 32, self.PS[b][0:32, :], b)
            if g.rope:
                b2 = self.bank()
                projB(416, 32, self.PS[b2][0:32, :], b2)
                self.tt("dve", rtmp[0:32, 0, :], self.PS[b][0:32, :], ropeC[0:32, :], ALU.mult, r=[("ps", b), "ropeCk"], w=[("rtmpk", 0)])
                self.tt("dve", rtmp[0:32, 1, :], self.PS[b2][0:32, :], ropeS[0:32, :], ALU.mult, r=[("ps", b2), "ropeSk"], w=[("rtmpk", 1)])
                self.tt("pool", krT, rtmp[0:32, 0, :], rtmp[0:32, 1, :], ALU.add, r=[("rtmpk", 0), ("rtmpk", 1)], w=[("krT", 0)])
            else:
                self.cp("dve", krT, self.PS[b][0:32, :], r=[("ps", b)], w=[("krT", 0)])
            k0 = g.ctx + t * TT
            for h in range(8):
                self.dma("pool", g.Kf[h, 64:96, k0:k0 + TT], krT, r=[("krT", 0)], w=[("Kf", h, k0, 1)], sem="krT_st")
            b = self.bank()
            projB(448, 64, self.PS[b][0:64, :], b)
            self.cp("dve", lrT[0:16, :], self.PS[b][0:16, :], r=[("ps", b), "lrT"], w=[("lrT", 0)])
            self.cp("dve", lrT[32:48, :], self.PS[b][32:48, :], r=[("ps", b), "lrT"], w=[("lrT", 1)])
            self.dma("pool", g.lr[:, t * TT:(t + 1) * TT], lrT, r=[("lrT", 0), ("lrT", 1), "lrT"], w=[("lr", t)], sem="lrT_st")
            self.ck("a5")
            for h in range(4):
                b = self.bank()
                projB(512 + h * 128, 128, self.PS[b], b)
                self.act(gqs[:, h, :], self.PS[b], AF.Copy, r=[("ps", b)], w=[("gqs", h)], scale=128.0 ** -0.5)
            self.dma("pool", g.gqT[:, :, t * TT:(t + 1) * TT].rearrange("h p t -> p h t"), gqs, r=[("gqs", h) for h in range(4)], w=[("gqT", t)], sem="gqs_st")
            for h in range(4):
                b = self.bank()
                projB(1024 + h * 128, 128, self.PS[b], b)
                self.cp("dve", gks[:, h, :], self.PS[b], r=[("ps", b)], w=[("gks", h)])
            self.dma("pool", g.gkT[:, :, t * TT:(t + 1) * TT].rearrange("h p t -> p h t"), gks, r=[("gks", h) for h in range(4)], w=[("gkT", t)], sem="gks_st")
            if t + 1 < g.nt:
                head(t + 1)
            def projA(s, col0, n, ps_ap, b):
                for kc in range(DC):
                    self.mm(ps_ap, hT[:, kc, s * 128:(s + 1) * 128], W[:, kc, col0:col0 + n], start=(kc == 0), stop=(kc == DC - 1), r=["W", hk[kc]], w=[("ps", b)])
            for s in range(4):
                tok0 = t * TT + s * 128
                b = self.bank()
                projA(s, 1536, 288, self.PS[b][:, 0:288], b)
                kv_ = kvns[s]
                ss_ = sss[s]
                self.act(junk, self.PS[b][:, 0:256], AF.Square, r=[("ps", b)], w=["junk", ("ss", s)], accum=ss_[:, 0:1])
                self.act(ss_[:, 1:2], ss_[:, 0:1], AF.Sqrt, r=[("ss", s), "eps"], w=[("ss1", s)], scale=1.0 / 256, bias=self.eps_t)
                self.recip(ss_[:, 1:2], ss_[:, 1:2], r=[("ss1", s)], w=[("ss1", s)])
                self.stt("dve", kv_[:, 0:256], self.PS[b][:, 0:256], ss_[:, 1:2], self.g_kv_bc, ALU.mult, ALU.mult, r=[("ps", b), ("ss1", s), "g_kv_bc"], w=[("kvn", s)])
                if g.prompt:
                    self.cp("act", kv_[:, 256:288], self.PS[b][:, 256:288], r=[("ps", b)], w=[("kvn2", s)])
                    self.dma("pool", self.o_ckv[tok0:tok0 + 128, :], kv_[:, 0:256], r=[("kvn", s)], sem="x")
                    self.dma("pool", self.o_kr[tok0:tok0 + 128, :], kv_[:, 256:288], r=[("kvn2", s)], sem="x")
            for s in range(4):
                tok0 = t * TT + s * 128
                ti = tok0 // 128
                tb = tokb[s % 2]
                for i in range(3):
                    b = self.bank()
                    projA(s, 1824 + i * 512, 512, self.PS[b], b)
                    if i == 2:
                        self.act(tb[:, i, :], self.PS[b], AF.Silu, r=[("ps", b)], w=[("tokb", s % 2, i)])
                    else:
                        self.cp("dve" if i == 0 else "act", tb[:, i, :], self.PS[b], r=[("ps", b)], w=[("tokb", s % 2, i)])
                self.dma("pool", g.gk_tok[ti], tb[:, 0, :], r=[("tokb", s % 2, 0)], w=[("gk_tok", ti)], sem="x")
                self.dma("pool", g.gv_tok[ti], tb[:, 1, :], r=[("tokb", s % 2, 1)], w=[("gv_tok", ti)], sem="x")
                self.dma("pool", g.go_tok[ti], tb[:, 2, :], r=[("tokb", s % 2, 2)], w=[("go_tok", ti)], sem="x")
            for s in range(4):
                b2 = self.bank()
                for kc in range(2):
                    self.tr(self.PS[b2][:, kc * 128:(kc + 1) * 128], kvns[s][:, kc * 128:(kc + 1) * 128], self.ident_f, r=[("kvn", s), "ident_f"], w=[("ps", b2)])
                self.cp("dve", ckvT[:, :, s * 128:(s + 1) * 128], self.PS[b2][:, 0:256].rearrange("p (k t) -> p k t", k=2), r=[("ps", b2)], w=[("ckvT", s)])
            self.ck("a7")
            kv_path([("ckvT", s) for s in range(4)], g.ctx + t * TT, TT)
            self.ck("a8")

    def attn_phase(self, g, layer):
        self.phase_begin()
        A = self.AR
        mla = (layer == 0)
        if mla and g.gi == 0:
            self.cast(self.wb_qkv, self.w_qkv, 8, 5120, "wb_qkv")
            self.cast(self.wb_o1, self.w_o1, 8, D, "wb_o1")
            self.cast(self.wb_f1[1], self.w_f1[1], 8, 2 * FH, "wb_f11")
            self.cast(self.wb_f2[1], self.w_f2[1], FC, D, "wb_f21")
        rows = 96 if mla else 128
        dv = 64 if mla else 128
        nm = 1 if mla else 2
        scale = (96.0 ** -0.5) if mla else (64.0 ** -0.5)
        nkt_all = g.nk // 128
        nbuf = 2
        KT = [A.alloc([128, g.nk], BF16) for _ in range(nbuf)]
        QTt = [A.alloc([128, g.ntok], BF16) for _ in range(nbuf)]
        VT = [A.alloc([128, nkt_all, 128], BF16) for _ in range(nbuf)]
        NE = 8
        E = [A.alloc([128, 2, TT], BF16) for _ in range(NE)]
        NZ = 6 if not mla else 1
        Zacc = [[A.alloc([128, 2, TT], F32) for _ in range(NZ)] for _ in range(2)]
        rz = [A.alloc([128, TT], F32) for _ in range(2)]
        o0 = A.alloc([128, TT], F32)
        o1 = A.alloc([128, TT], F32)
        osq = A.alloc([128, TT], BF16)
        orstd = A.alloc([128, TT], F32)
        mixo = [A.alloc([128, TT], BF16) for _ in range(2)]
        oraw = [[A.alloc([128, TT], F32) for _ in range(2)] for _ in range(2)]
        pending_fin = [None, None, None]
        SL = [0, 1, 2]
        OB = [6, 7]
        ZB = [4, 5]
        if mla:
            for s in range(nbuf):
                self.memset("pool", VT[s][:, :, 64:128], 1.0, w=[("VTones", s)])
        gi = 0
        mi = 0
        zi = 0
        for h in range(8):
            s = h % nbuf
            self.dma("sp", KT[s][0:rows, :], g.Kf[h, 0:rows, :], w=[("KT", s)], sem="x")
            self.dma("sp", QTt[s][0:rows, :], g.Qf[h, 0:rows, :], w=[("QT", s)], sem="x")
            self.dma("sp", VT[s][:, :, 0:dv], g.Vf[h, :, :, 0:dv], w=[("VT", s)], sem="x")
            vkeys = [("VT", s)] + ([("VTones", s)] if mla else [])
            for (q0, qn, k0, kn) in g.att:
                nkt = kn // 128
                kt0 = k0 // 128
                items = [(kt, m) for kt in range(nkt) for m in range(nm)]
                groups = [items[i:i + 2] for i in range(0, len(items), 2)]
                zpar = zi % 2
                zi += 1
                state = {"gidx": 0, "used": [False] * NZ, "cnt": [0, 0]}

                def issue_qk(grp):
                    nonlocal gi
                    sl = SL[gi % len(SL)]
                    e = gi % NE
                    gi += 1
                    pk = [("ps", 2 * sl), ("ps", 2 * sl + 1)]
                    for j, (kt, m) in enumerate(grp):
                        r0, r1 = (0, rows) if mla else (m * 64, (m + 1) * 64)
                        kk = (kt0 + kt) * 128
                        self.mm(self.PS[2 * sl + j][:, 0:qn], KT[s][r0:r1, kk:kk + 128], QTt[s][r0:r1, q0:q0 + qn], r=[("KT", s), ("QT", s)], w=pk)
                    n = len(grp)
                    src = self.PSP[sl].rearrange("p (j n) -> p j n", j=2)[:, 0:n, 0:qn]
                    self.act(E[e][:, 0:n, 0:qn], src, AF.Exp, r=pk, w=[("E", e)], scale=scale)
                    if not mla:
                        on_dma = (state["gidx"] % 2 == 1)
                        state["gidx"] += 1
                        w_ = 1 if on_dma else 0
                        if on_dma:
                            zi_ = 2 + (state["cnt"][1] % 4)
                        else:
                            zi_ = state["cnt"][0] % 2
                        state["cnt"][w_] += 1
                        zc = Zacc[zpar][zi_]
                        zk = ("Zacc", zpar, zi_)
                        firstuse = not state["used"][zi_]
                        state["used"][zi_] = True
                        if on_dma:
                            self.dma("pool", zc[:, :, 0:qn], E[e][:, :, 0:qn], r=[("E", e)], w=[zk], sem="x", accum=not firstuse)
                        elif firstuse:
                            self.cp("dve", zc[:, :, 0:qn], E[e][:, :, 0:qn], r=[("E", e)], w=[zk])
                        else:
                            self.tt("dve", zc[:, :, 0:qn], zc[:, :, 0:qn], E[e][:, :, 0:qn], ALU.add, r=[("E", e), zk], w=[zk])
                    return (e, False)

                def issue_pv(grp, eo):
                    e, on_pe = eo
                    for j, (kt, m) in enumerate(grp):
                        st_ = (kt == 0)
                        sp_ = (kt == nkt - 1)
                        M = 128
                        self.mm(self.PS[OB[m]][0:M, 0:qn], VT[s][:, kt0 + kt, 0:M], E[e][:, j, 0:qn], start=st_, stop=sp_, r=vkeys + [("E", e)], w=[("ps", OB[m])])
                    if not mla and on_pe:
                        zf = (state["zpe"] == 0)
                        state["zpe"] += 1
                        zl = (state["zpe"] == state["npe"])
                        for j, (kt, m) in enumerate(grp):
                            self.mm(self.PS[ZB[m]][:, 0:qn], self.ones_b, E[e][:, j, 0:qn], start=zf, stop=zl, r=["ones_b", ("E", e)], w=[("ps", ZB[m])])

                LA = 2
                pend = []
                for gidx_, grp in enumerate(groups):
                    pend.append((grp, issue_qk(grp)))
                    if len(pend) > LA:
                        issue_pv(*pend.pop(0))
                    for st_i, at in enumerate((3, 12, 22)):
                        if gidx_ == at and pending_fin[st_i] is not None:
                            nxt = pending_fin[st_i]()
                            pending_fin[st_i] = None
                            if st_i + 1 < 3:
                                pending_fin[st_i + 1] = nxt
                while pend:
                    issue_pv(*pend.pop(0))
                for st_i in range(3):
                    if pending_fin[st_i] is not None:
                        nxt = pending_fin[st_i]()
                        pending_fin[st_i] = None
                        if st_i + 1 < 3:
                            pending_fin[st_i + 1] = nxt
                par = mi % 2
                mi += 1
                oraw0, oraw1 = oraw[par]
                self.cp("dve", oraw0[:, 0:qn], self.PS[OB[0]][:, 0:qn], r=[("ps", OB[0])], w=[("oraw", par, 0)])
                if not mla:
                    self.cp("dve", oraw1[:, 0:qn], self.PS[OB[1]][:, 0:qn], r=[("ps", OB[1])], w=[("oraw", par, 1)])
                usedz = [i for i in range(NZ) if state["used"][i]] if not mla else []

                def fin(h=h, q0=q0, qn=qn, par=par, zpar=zpar, usedz=usedz, oraw0=oraw0, oraw1=oraw1):
                    nonlocal gi
                    mo = mixo[par]
                    mk = ("mixo", par)
                    tile_i = q0 // TT
                    qo = q0 % TT
                    if mla:
                        sl = SL[gi % len(SL)]
                        gi += 1
                        pk = [("ps", 2 * sl), ("ps", 2 * sl + 1)]
                        self.mm(self.PS[2 * sl][0:64, 0:qn], self.ident_f[:, 64:128], oraw0[:, 0:qn], r=[("oraw", par, 0), "ident_f"], w=pk)
                        self.act(rz[0][0:64, 0:qn], self.PS[2 * sl][0:64, 0:qn], AF.Ln, r=pk, w=[("rz", 0)])
                        self.act(rz[0][0:64, 0:qn], rz[0][0:64, 0:qn], AF.Exp, r=[("rz", 0)], w=[("rz", 0)], scale=-1.0)
                        self.tt("dve", mo[0:64, 0:qn], oraw0[0:64, 0:qn], rz[0][0:64, 0:qn], ALU.mult, r=[("oraw", par, 0), ("rz", 0)], w=[mk])
                        pr = (h % 2) * 64
                        self.dma("pool", g.mixT[tile_i, pr:pr + 64, h // 2, qo:qo + qn], mo[0:64, 0:qn], r=[mk], w=[("mixT", tile_i, h)], sem="x")
                        return None
                    dz = [i for i in usedz if i >= 2]
                    while len(dz) > 1:
                        a_, b_ = dz[0], dz[1]
                        self.tt("dve", Zacc[zpar][a_][:, :, 0:qn], Zacc[zpar][a_][:, :, 0:qn], Zacc[zpar][b_][:, :, 0:qn], ALU.add,
                                r=[("Zacc", zpar, a_), ("Zacc", zpar, b_)], w=[("Zacc", zpar, a_)])
                        dz = dz[2:] + [a_]
                    usedz = [i for i in usedz if i < 2] + dz
                    return lambda: finB(usedz)

                def finB(usedz, h=h, q0=q0, qn=qn, par=par, zpar=zpar, oraw0=oraw0, oraw1=oraw1):
                    nonlocal gi
                    mo = mixo[par]
                    mk = ("mixo", par)
                    tile_i = q0 // TT
                    qo = q0 % TT
                    sl = SL[gi % len(SL)]
                    gi += 1
                    pk = [("ps", 2 * sl), ("ps", 2 * sl + 1)]
                    for m in range(2):
                        for ii, i in enumerate(usedz):
                            self.mm(self.PS[2 * sl + m][:, 0:qn], self.ones_f, Zacc[zpar][i][:, m, 0:qn], start=(ii == 0), stop=(ii == len(usedz) - 1),
                                    r=[("Zacc", zpar, i), "ones_f"], w=pk)
                    for m in range(2):
                        self.act(rz[m][:, 0:qn], self.PS[2 * sl + m][:, 0:qn], AF.Ln, r=pk, w=[("rz", m)])
                        self.act(rz[m][:, 0:qn], rz[m][:, 0:qn], AF.Exp, r=[("rz", m)], w=[("rz", m)], scale=-1.0)
                    self.tt("dve", o0[:, 0:qn], oraw0[:, 0:qn], rz[0][:, 0:qn], ALU.mult, r=[("oraw", par, 0), ("rz", 0)], w=["o0"])
                    self.tt("dve", o1[:, 0:qn], oraw1[:, 0:qn], rz[1][:, 0:qn], ALU.mult, r=[("oraw", par, 1), ("rz", 1)], w=["o1"])
                    self.stt("dve", o0[:, 0:qn], o1[:, 0:qn], self.lam_t[:, 0:1], o0[:, 0:qn], ALU.mult, ALU.add, r=["o0", "o1", "lam_t0"], w=["o0"])
                    self.tt("pool", osq[:, 0:qn], o0[:, 0:qn], o0[:, 0:qn], ALU.mult, r=["o0"], w=["osq"])

                    def fin2():
                        nonlocal gi
                        sl = SL[gi % len(SL)]
                        gi += 1
                        pk = [("ps", 2 * sl), ("ps", 2 * sl + 1)]
                        self.mm(self.PS[2 * sl][:, 0:qn], self.ones_b, osq[:, 0:qn], r=["osq", "ones_b"], w=pk)
                        self.rstd_from(orstd[:, 0:qn], self.PS[2 * sl][:, 0:qn], 128, r=pk + ["eps"], w=["orstd"])
                        self.stt("dve", mo[:, 0:qn], o0[:, 0:qn], self.lam_t[:, 1:2], orstd[:, 0:qn], ALU.mult, ALU.mult, r=["o0", "orstd", "lam_t1"], w=[mk])
                        self.dma("pool", g.mixT[tile_i, :, h, qo:qo + qn], mo[:, 0:qn], r=[mk], w=[("mixT", tile_i, h)], sem="x")
                    return fin2
                    sl = SL[gi % len(SL)]
                    gi += 1
                    pk = [("ps", 2 * sl), ("ps", 2 * sl + 1)]
                    self.mm(self.PS[2 * sl][:, 0:qn], self.ones_b, osq[:, 0:qn], r=["osq", "ones_b"], w=pk)
                    self.rstd_from(orstd[:, 0:qn], self.PS[2 * sl][:, 0:qn], 128, r=pk + ["eps"], w=["orstd"])
                    self.stt("dve", mo[:, 0:qn], o0[:, 0:qn], self.lam_t[:, 1:2], orstd[:, 0:qn], ALU.mult, ALU.mult, r=["o0", "orstd", "lam_t1"], w=[mk])
                    self.dma("pool", g.mixT[tile_i, :, h, qo:qo + qn], mo[:, 0:qn], r=[mk], w=[("mixT", tile_i, h)], sem="x")

                pending_fin[0] = fin
        for st_i in range(3):
            if pending_fin[st_i] is not None:
                nxt = pending_fin[st_i]()
                pending_fin[st_i] = None
                if st_i + 1 < 3:
                    pending_fin[st_i + 1] = nxt

    def gla_phase(self, g):
        self.phase_begin()
        A = self.AR
        self.pool_banks = list(range(8))
        NB = 2
        lr = A.alloc([64, g.ntok], F32)
        self.dma("sp", lr, g.lr, w=["lr"], sem="x")
        nseq = len(g.seqs)
        S = [[A.alloc([128, 4, 128], F32) for _ in range(2)] for _ in range(nseq)]
        Sb = [[A.alloc([128, 4, 128], BF16) for _ in range(2)] for _ in range(nseq)]
        ld = {}
        for nm_, shape, dt in (("qT", [128, 4, 128], BF16), ("kT", [128, 4, 128], BF16), ("kt", [128, 512], BF16), ("vt", [128, 512], BF16),
                               ("e", [128, 512], F32), ("y", [128, 512], F32), ("sufe", [128, 512], F32), ("kdec", [128, 512], BF16),
                               ("Ep", [128, 4, 128], F32), ("Em", [128, 4, 128], F32), ("qin", [128, 4, 128], BF16), ("kin", [128, 4, 128], BF16),
                               ("AT", [128, 4, 128], BF16), ("o", [128, 512], F32), ("ofl", [128, 512], F32), ("got", [128, 512], BF16)):
            ld[nm_] = [[A.alloc(shape, dt) for _ in range(NB)] for _ in range(2)]
        ssq = A.alloc([128, 4], F32)
        rs = A.alloc([128, 4], F32)
        junk = A.alloc([128, 128], F32)
        gl = A.alloc([128, 512], F32)
        glb = A.alloc([128, 512], BF16)
        glT = A.alloc([128, 4, 128], BF16)

        for si, (t0, L) in enumerate(g.seqs):
            for d in range(2):
                if g.prompt:
                    self.memset("pool", S[si][d], 0.0, w=[("S", si, d)])
                else:
                    src = self.sf_in if d == 0 else self.sb_in
                    self.dma("sp", S[si][d], src.rearrange("h k v -> k h v"), w=[("S", si, d)], sem="x")
                self.cp("act", Sb[si][d], S[si][d], r=[("S", si, d)], w=[("Sb", si, d)])
        cnt = [0, 0]
        done_tiles = set()

        def f1(d, si, i):
            t0, L = g.seqs[si]
            n = cnt[d] % NB
            cnt[d] += 1
            B = lambda nm_: ld[nm_][d][n]
            Kk = lambda nm_: (nm_, d, n)
            tok0 = t0 + i * 128
            ti = tok0 // 128
            prow = 0 if d == 0 else 32
            self.dma("sp", B("qT"), g.gqT[:, :, tok0:tok0 + 128].rearrange("h p t -> p h t"), w=[Kk("qT")], sem="x")
            self.dma("sp", B("kT"), g.gkT[:, :, tok0:tok0 + 128].rearrange("h p t -> p h t"), w=[Kk("kT")], sem="x")
            self.dma("sp", B("kt"), g.gk_tok[ti], w=[Kk("kt")], sem="x")
            self.dma("sp", B("vt"), g.gv_tok[ti], w=[Kk("vt")], sem="x")
            b = self.bank()
            self.mm(self.PS[b], lr[prow:prow + 17, tok0:tok0 + 128], self.wup[prow:prow + 17, :], r=["lr", "wup0", "wup1", "wup2", "wup3"], w=[("ps", b)])
            self.act(B("e"), self.PS[b], AF.Exp, r=[("ps", b)], w=[Kk("e")], scale=-1.0)
            self.act(B("y"), B("e"), AF.Ln, r=[Kk("e")], w=[Kk("y")], bias=1.0)
            first = (si, i) not in done_tiles
            done_tiles.add((si, i))
            return (d, si, i, n, first)

        def f2(tok):
            d, si, i, n, first = tok
            B = lambda nm_: ld[nm_][d][n]
            Kk = lambda nm_: (nm_, d, n)
            triC = self.triLE if d == 0 else self.triGE
            triS = self.triGT if d == 0 else self.triLT
            tk = ("triLE" if d == 0 else "triGE")
            tks = ("triGT" if d == 0 else "triLT")
            b = self.bank()
            self.mm(self.PS[b], triS, B("y"), r=[tks, Kk("y")], w=[("ps", b)])
            self.act(B("sufe"), self.PS[b], AF.Exp, r=[("ps", b)], w=[Kk("sufe")], scale=-1.0 / 16)
            self.tt("dve", B("kdec"), B("sufe"), B("kt"), ALU.mult, r=[Kk("sufe"), Kk("kt")], w=[Kk("kdec")])
            b = self.bank()
            for h in range(4):
                self.mm(self.PS[b][:, h * 128:(h + 1) * 128], B("y")[:, h * 128:(h + 1) * 128], triC, r=[tk, Kk("y")], w=[("ps", b)])
            pv = self.PS[b].rearrange("p (h c) -> p h c", h=4)
            self.act(B("Ep"), pv, AF.Exp, r=[("ps", b)], w=[Kk("Ep")], scale=-1.0 / 16)
            self.act(B("Em"), pv, AF.Exp, r=[("ps", b)], w=[Kk("Em")], scale=1.0 / 16)
            self.tt("dve", B("qin"), B("qT"), B("Ep"), ALU.mult, r=[Kk("qT"), Kk("Ep")], w=[Kk("qin")])
            self.tt("pool", B("kin"), B("kT"), B("Em"), ALU.mult, r=[Kk("kT"), Kk("Em")], w=[Kk("kin")])

        def f3(tok):
            d, si, i, n, first = tok
            B = lambda nm_: ld[nm_][d][n]
            Kk = lambda nm_: (nm_, d, n)
            triC = self.triLE if d == 0 else self.triGE
            tk = ("triLE" if d == 0 else "triGE")
            b = self.bank()
            for h in range(4):
                self.mm(self.PS[b][:, h * 128:(h + 1) * 128], B("kin")[:, h, :], B("qin")[:, h, :], r=[Kk("kin"), Kk("qin")], w=[("ps", b)])
            self.tt("dve", B("AT"), self.PS[b].rearrange("p (h c) -> p h c", h=4), bc_mid(triC, 4), ALU.mult, r=[("ps", b), tk], w=[Kk("AT")])

        def back(tok):
            d, si, i, n, first = tok
            t0, L = g.seqs[si]
            B = lambda nm_: ld[nm_][d][n]
            Kk = lambda nm_: (nm_, d, n)
            tok0 = t0 + i * 128
            ti = tok0 // 128
            Sd, Sbd = S[si][d], Sb[si][d]
            if not first:
                self.dma("sp", B("ofl"), g.of[ti], r=[("of", ti)], w=[Kk("ofl")], sem="x")
                self.dma("sp", B("got"), g.go_tok[ti], w=[Kk("got")], sem="x")
            bo = self.bank()
            for h in range(4):
                self.mm(self.PS[bo][:, h * 128:(h + 1) * 128], B("AT")[:, h, :], B("vt")[:, h * 128:(h + 1) * 128], start=True, stop=False, r=[Kk("AT"), Kk("vt")], w=[("ps", bo)])
                self.mm(self.PS[bo][:, h * 128:(h + 1) * 128], B("qin")[:, h, :], Sbd[:, h, :], start=False, stop=True, r=[Kk("qin"), ("Sb", si, d)], w=[("ps", bo)])
            bu = self.bank()
            for h in range(4):
                self.mm(self.PS[bu][:, h * 128:(h + 1) * 128], B("kdec")[:, h * 128:(h + 1) * 128], B("vt")[:, h * 128:(h + 1) * 128], r=[Kk("kdec"), Kk("vt")], w=[("ps", bu)])
            col = 127 if d == 0 else 0
            for h in range(4):
                self.stt("dve", Sd[:, h, :], Sd[:, h, :], B("Ep")[:, h, col:col + 1], self.PS[bu][:, h * 128:(h + 1) * 128], ALU.mult, ALU.add,
                         r=[("S", si, d), Kk("Ep"), ("ps", bu)], w=[("S", si, d)])
            self.cp("act", Sbd, Sd, r=[("S", si, d)], w=[("Sb", si, d)])
            ob = B("o")
            if first:
                self.cp("act", ob, self.PS[bo], r=[("ps", bo)], w=[Kk("o")])
                self.dma("pool", g.of[ti], ob, r=[Kk("o")], w=[("of", ti)], sem="x")
                return
            self.tt("dve", ob, self.PS[bo], B("ofl"), ALU.add, r=[("ps", bo), Kk("ofl")], w=[Kk("o")])
            for h in range(4):
                self.act(junk, ob[:, h * 128:(h + 1) * 128], AF.Square, r=[Kk("o")], w=["gjunk", ("ssq", h)], accum=ssq[:, h:h + 1])
            self.act(rs, ssq, AF.Sqrt, r=[("ssq", h) for h in range(4)] + ["eps"], w=["rs"], scale=1.0 / 128, bias=self.eps_t)
            self.recip(rs, rs, r=["rs"], w=["rs"])
            for h in range(4):
                self.stt("dve", gl[:, h * 128:(h + 1) * 128], ob[:, h * 128:(h + 1) * 128], rs[:, h:h + 1], self.g_gla_bc, ALU.mult, ALU.mult,
                         r=[Kk("o"), "rs", "g_gla_bc"], w=[("gl", h)])
            self.tt("pool", glb, gl, B("got"), ALU.mult, r=[("gl", h) for h in range(4)] + [Kk("got")], w=["glb"])
            b = self.bank()
            pb = self.PS[b].bitcast(BF16)
            for h in range(4):
                self.tr(pb[:, h * 128:(h + 1) * 128], glb[:, h * 128:(h + 1) * 128], self.ident_b, r=["glb", "ident_b"], w=[("ps", b)])
            self.cp("dve", glT, pb[:, 0:512].rearrange("p (h t) -> p h t", h=4), r=[("ps", b)], w=["glT"])
            tile_i = tok0 // TT
            qo = tok0 % TT
            self.dma("pool", g.mixT[tile_i, :, 4:8, qo:qo + 128], glT, r=["glT"], w=[("mixT", tile_i, "g", qo)], sem="x")

        for si, (t0, L) in enumerate(g.seqs):
            nt_ = L // 128
            chains = [[(0, si, i) for i in range(nt_)], [(1, si, i) for i in reversed(range(nt_))]]
            pend = [None, None]
            for k in range(nt_):
                toks = [f1(*chains[0][k]), f1(*chains[1][k])]
                if pend[0] is not None:
                    back(pend[0])
                f2(toks[0])
                f2(toks[1])
                if pend[1] is not None:
                    back(pend[1])
                f3(toks[0])
                f3(toks[1])
                pend = toks
            back(pend[0])
            back(pend[1])
        if g.prompt:
            for si in range(nseq):
                self.dma("pool", self.o_sf[si].rearrange("h k v -> k h v"), S[si][0], r=[("S", si, 0)], sem="x")
                self.dma("pool", self.o_sb[si].rearrange("h k v -> k h v"), S[si][1], r=[("S", si, 1)], sem="x")

    def phaseD(self, l):
        self.phase_begin()
        A = self.AR
        self.pool_banks = list(range(8))
        Wo = A.alloc([128, 8, D], BF16)
        wo_src = self.wb_o0 if l == 0 else self.wb_o1
        self.dma("sp", Wo, wo_src, w=["Wo"], sem="x")
        NS = 3
        W1 = [A.alloc([128, 8, 512], BF16) for _ in range(NS)]
        NW2 = 3
        W2 = [A.alloc([128, FC, 128], BF16) for _ in range(NW2)]
        xT = [A.alloc([128, DC, TT], F32) for _ in range(3)]
        hT = [A.alloc([128, DC, TT], BF16) for _ in range(2)]
        mix = A.alloc([128, DC, TT], BF16)
        osb = A.alloc([128, DC, TT], F32)
        sq = A.alloc([128, DC, TT], BF16)
        tmp = A.alloc([128, DC, TT], F32)
        rstd = A.alloc([128, TT], F32)
        actT = A.alloc([128, FC, TT], BF16)
        sg = [A.alloc([128, TT], BF16) for _ in range(2)]
        cnt = {"w1": 0, "w2": 0}
        items = [(g_, t_) for g_ in self.G for t_ in range(g_.nt)]
        NI = len(items)

        def stats(n_key):
            b = self.bank()
            for c in range(DC):
                self.mm(self.PS[b], self.ones_b, sq[:, c, :], start=(c == 0), stop=(c == DC - 1), r=[("sq", c), "ones_b"], w=[("ps", b)])
            self.rstd_from(rstd, self.PS[b], D, r=[("ps", b), "eps"], w=["rstd"])

        def residual(p, kind, gi_):
            for c in range(DC):
                self.stt("dve", tmp[:, c, :], osb[:, c, :], self.sc_a(l, gi_, kind, c), rstd, ALU.mult, ALU.mult,
                         r=[("osb", c), "rstd", ("sc", l, gi_, kind)], w=[("tmp", c)])
                self.tt("pool", xT[p][:, c, :], xT[p][:, c, :], tmp[:, c, :], ALU.add, r=[("xT", p, c), ("tmp", c)], w=[("xT", p, c)])

        def s1_load(k):
            g, t = items[k]
            p = k % 3
            self.dma("sp", xT[p], g.xT[t], w=[("xT", p, c) for c in range(DC)], sem="x")
            self.dma("sp", mix, g.mixT[t], w=["mix"], sem="x")

        def s1A(t):
            for oc in range(DC):
                b = self.bank()
                for kc in range(DC):
                    self.mm(self.PS[b], Wo[:, kc, oc * 128:(oc + 1) * 128], mix[:, kc, :], start=(kc == 0), stop=(kc == DC - 1), r=["Wo", "mix"], w=[("ps", b)])
                self.cp("act", osb[:, oc, :], self.PS[b], r=[("ps", b)], w=[("osb", oc)])
                self.tt("pool", sq[:, oc, :], osb[:, oc, :], osb[:, oc, :], ALU.mult, r=[("osb", oc)], w=[("sq", oc)])

        def s1B(t):
            stats(None)

        def s1C(k):
            p = k % 3
            residual(p, 1, items[k][0].gi)
            for c in range(DC):
                self.act(sq[:, c, :], xT[p][:, c, :], AF.Square, r=[("xT", p, c)], w=[("sq", c)])

        def s1D(t):
            stats(None)

        def s1E(k):
            p = k % 3
            ph = k % 2
            gi_ = items[k][0].gi
            for c in range(DC):
                self.stt("dve", tmp[:, c, :], xT[p][:, c, :], self.sc_a(l, gi_, 2, c), rstd, ALU.mult, ALU.mult,
                         r=[("xT", p, c), "rstd", ("sc", l, gi_, 2)], w=[("tmp", c)])
                self.act(hT[ph][:, c, :], tmp[:, c, :], AF.Identity, r=[("tmp", c), ("modpp", l)], w=[("hT", ph, c)],
                         bias=self.sc_shift(l, gi_, 3, c))

        def ffn_in_slab(k, j):
            p = k % 2
            s_ = cnt["w1"] % NS
            cnt["w1"] += 1
            self.dma("sp", W1[s_][:, :, 0:256], self.wb_f1[l][:, :, j * 256:(j + 1) * 256], w=[("W1", s_, 0)], sem="x")
            self.dma("sp", W1[s_][:, :, 256:512], self.wb_f1[l][:, :, FH + j * 256:FH + (j + 1) * 256], w=[("W1", s_, 1)], sem="x")
            for ii in range(2):
                i = j * 2 + ii
                bg = self.bank()
                for kc in range(DC):
                    self.mm(self.PS[bg], W1[s_][:, kc, ii * 128:(ii + 1) * 128], hT[p][:, kc, :], start=(kc == 0), stop=(kc == DC - 1), r=[("W1", s_, 0), ("hT", p, kc)], w=[("ps", bg)])
                bu = self.bank()
                for kc in range(DC):
                    self.mm(self.PS[bu], W1[s_][:, kc, 256 + ii * 128:256 + (ii + 1) * 128], hT[p][:, kc, :], start=(kc == 0), stop=(kc == DC - 1), r=[("W1", s_, 1), ("hT", p, kc)], w=[("ps", bu)])
                sgi = sg[i % 2]
                self.act(sgi, self.PS[bg], AF.Silu, r=[("ps", bg)], w=[("sg", i % 2)])
                self.tt("dve", actT[:, i, :], sgi, self.PS[bu], ALU.mult, r=[("sg", i % 2), ("ps", bu)], w=[("actT", i)])

        def ffn_out(t):
            for oc in range(DC):
                w_ = cnt["w2"] % NW2
                cnt["w2"] += 1
                self.dma("sp", W2[w_], self.wb_f2[l][:, :, oc * 128:(oc + 1) * 128], w=[("W2", w_)], sem="x")
                b = self.bank()
                for kc in range(FC):
                    self.mm(self.PS[b], W2[w_][:, kc, :], actT[:, kc, :], start=(kc == 0), stop=(kc == FC - 1), r=[("W2", w_), ("actT", kc)], w=[("ps", b)])
                self.cp("act", osb[:, oc, :], self.PS[b], r=[("ps", b)], w=[("osb", oc)])
                self.tt("pool", sq[:, oc, :], osb[:, oc, :], osb[:, oc, :], ALU.mult, r=[("osb", oc)], w=[("sq", oc)])

        def tailA(t):
            stats(None)

        def tailB(k):
            g, t = items[k]
            p = k % 3
            residual(p, 3, g.gi)
            if l == 0:
                self.dma("pool", g.xT[t], xT[p], r=[("xT", p, c) for c in range(DC)], w=[("xTs", g.name, t)], sem="x")

        def tailC(k):
            if l == 0:
                return
            g, t = items[k]
            p = k % 3
            yt = tmp.rearrange("p c t -> p (c t)").rearrange("p (s d) -> p s d", s=4)
            for s4 in range(4):
                for half in range(2):
                    b = self.bank()
                    for cc in range(4):
                        c = half * 4 + cc
                        self.tr(self.PS[b][:, cc * 128:(cc + 1) * 128], xT[p][:, c, s4 * 128:(s4 + 1) * 128], self.ident_f, r=[("xT", p, c), "ident_f"], w=[("ps", b)])
                    self.cp("act" if half else "dve", yt[:, s4, half * 512:(half + 1) * 512], self.PS[b], r=[("ps", b)], w=[("tmp", s4 * 2 + half)])
            self.dma("pool", g.y_out[t * TT:(t + 1) * TT, :].rearrange("(s p) d -> p s d", p=128), yt,
                     r=[("tmp", c) for c in range(DC)], sem="x")

        s1_load(0)
        s1A(0)
        s1B(0)
        s1C(0)
        s1D(0)
        s1E(0)
        for t in range(NI):
            hooks = {}
            if t > 0:
                hooks[0] = [lambda t=t: tailA(t - 1)]
                hooks[1] = [lambda t=t: tailB(t - 1)]
                hooks[4] = [lambda t=t: tailC(t - 1)]
            if t + 1 < NI:
                hooks.setdefault(2, []).append(lambda t=t: s1_load(t + 1))
                hooks.setdefault(3, []).append(lambda t=t: s1A(t + 1))
                hooks.setdefault(5, []).append(lambda t=t: s1B(t + 1))
                hooks.setdefault(6, []).append(lambda t=t: s1C(t + 1))
                hooks.setdefault(8, []).append(lambda t=t: s1D(t + 1))
                hooks.setdefault(9, []).append(lambda t=t: s1E(t + 1))
            for j in range(FC // 2):
                ffn_in_slab(t, j)
                for f in hooks.get(j, ()):
                    f()
            ffn_out(t)
        tailA(NI - 1)
        tailB(NI - 1)
        tailC(NI - 1)

    def phaseA1(self, g):
        self.phase_begin()
        A = self.AR
        l = 1
        W = A.alloc([128, 8, 5120], BF16)
        self.dma("sp", W, self.wb_qkv, w=["W"], sem="W")
        xT = A.alloc([128, DC, TT], F32)
        hTs = [A.alloc([128, DC, TT], BF16) for _ in range(2)]
        sq = A.alloc([128, DC, TT], BF16)
        tmp = A.alloc([128, DC, TT], F32)
        rstd = A.alloc([128, TT], F32)
        ropeC = A.alloc([128, TT], F32)
        ropeS = A.alloc([128, TT], F32)
        rtmp = A.alloc([128, 2, TT], F32)
        qk = [A.alloc([128, TT], BF16) for _ in range(3)]
        qbf = [A.alloc([128, TT], BF16) for _ in range(2)]
        vtok = [A.alloc([128, D], BF16) for _ in range(2)]
        ftok = [A.alloc([128, D], F32) for _ in range(2)]
        cx = A.alloc([128, D], F32)
        qi = 0
        if g.ctx:
            for s in range(g.ctx // 128):
                self.dma("sp", cx, self.dk_c[s * 128:(s + 1) * 128, :], w=["cx"], sem="cx")
                for half in range(2):
                    b = self.bank()
                    for cc in range(4):
                        c = half * 4 + cc
                        self.tr(self.PS[b][:, cc * 128:(cc + 1) * 128], cx[:, c * 128:(c + 1) * 128], self.ident_f, r=["cx", "ident_f"], w=[("ps", b)])
                    kq = qk[qi % 3]
                    kqk = ("qk", qi % 3)
                    qi += 1
                    self.cp("act" if half else "dve", kq, self.PS[b], r=[("ps", b)], w=[kqk])
                    self.dma("pool", g.Kf[half * 4:half * 4 + 4, :, s * 128:(s + 1) * 128].rearrange("h p t -> p h t"),
                             kq.rearrange("p (h t) -> p h t", h=4), r=[kqk], w=[("Kf", "c", s, half)], sem=f"qk{(qi - 1) % 3}")
                self.ck("b0")
                self.dma("pool", g.Vf[:, :, s, :].rearrange("h p e -> p h e"), self.dv_c[s * 128:(s + 1) * 128, :].rearrange("p (h e) -> p h e", h=8),
                         w=[("Vf", s)], sem="cast")
        def head(t):
            self.dma("sp", xT, g.xT[t], w=[("xT", c) for c in range(DC)], sem="xT_ld")
            self.norm_mod(xT, hTs[t % 2], l, g, 0, sq, rstd, tmp, [("xT", c) for c in range(DC)], hkeys=[("hT", t % 2, c) for c in range(DC)])

        def rope_load(t):
            if g.rope:
                self.dma("sp", ropeC, self.ropeD[0, :, t * TT:(t + 1) * TT], w=["ropeC"], sem="ropeC")
                self.dma("sp", ropeS, self.ropeD[1, :, t * TT:(t + 1) * TT], w=["ropeS"], sem="ropeS")

        head(0)
        for t in range(g.nt):
            hT = hTs[t % 2]
            hk = [("hT", t % 2, c) for c in range(DC)]
            rope_load(t)

            def projB(col0, b):
                for kc in range(DC):
                    self.mm(self.PS[b], W[:, kc, col0:col0 + 128], hT[:, kc, :], start=(kc == 0), stop=(kc == DC - 1), r=["W", hk[kc]], w=[("ps", b)])

            for which in range(2):
                base = 0 if which == 0 else 2048
                for h in range(8):
                    b = self.bank()
                    projB(base + h * 128, b)
                    o_ = qk[qi % 3]
                    ok = ("qk", qi % 3)
                    qi += 1
                    if g.rope:
                        qb_ = qbf[qi % 2]
                        self.cp("act", qb_, self.PS[b], r=[("ps", b)], w=[("qbf", qi % 2)])
                        b2 = self.bank()
                        self.mm(self.PS[b2], self.permB, qb_, r=["permB", ("qbf", qi % 2)], w=[("ps", b2)])
                        self.tt("dve", rtmp[:, 0, :], self.PS[b], ropeC, ALU.mult, r=[("ps", b), "ropeC"], w=[("rtmp", 0)])
                        self.tt("dve", rtmp[:, 1, :], self.PS[b2], ropeS, ALU.mult, r=[("ps", b2), "ropeS"], w=[("rtmp", 1)])
                        self.tt("pool", o_, rtmp[:, 0, :], rtmp[:, 1, :], ALU.add, r=[("rtmp", 0), ("rtmp", 1)], w=[ok])
                    else:
                        self.cp("act", o_, self.PS[b], r=[("ps", b)], w=[ok])
                    if which == 0:
                        self.dma("pool", g.Qf[h, :, t * TT:(t + 1) * TT], o_, r=[ok], w=[("Qf", h, t)], sem=f"qk{(qi - 1) % 3}")
                    else:
                        k0 = g.ctx + t * TT
                        self.dma("pool", g.Kf[h, :, k0:k0 + TT], o_, r=[ok], w=[("Kf", h, t)], sem=f"qk{(qi - 1) % 3}")
            if t + 1 < g.nt:
                head(t + 1)
            for s in range(4):
                tok0 = t * TT + s * 128
                kt = (g.ctx + tok0) // 128
                vt_ = vtok[s % 2]
                for half in range(2):
                    b = self.bank()
                    for kc in range(DC):
                        self.mm(self.PS[b], hT[:, kc, s * 128:(s + 1) * 128], W[:, kc, 4096 + half * 512:4096 + (half + 1) * 512], start=(kc == 0), stop=(kc == DC - 1), r=["W", hk[kc]], w=[("ps", b)])
                    self.cp("act" if half else "dve", vt_[:, half * 512:(half + 1) * 512], self.PS[b], r=[("ps", b)], w=[("vtok", s % 2, half)])
                    if g.prompt:
                        ft = ftok[0]
                        self.cp("dve" if half else "act", ft[:, half * 512:(half + 1) * 512], self.PS[b], r=[("ps", b)], w=[("ftok", 0, half)])
                self.dma("pool", g.Vf[:, :, kt, :].rearrange("h p e -> p h e"), vt_.rearrange("p (h e) -> p h e", h=8), r=[("vtok", s % 2, 0), ("vtok", s % 2, 1)], w=[("Vf", kt)], sem=f"vtok{s % 2}")
                if g.prompt:
                    self.dma("pool", self.o_dv[tok0:tok0 + 128, :], ftok[0], r=[("ftok", 0, 0), ("ftok", 0, 1)], sem="ftok0")
                    ft = ftok[1]
                    for half in range(2):
                        b = self.bank()
                        for kc in range(DC):
                            self.mm(self.PS[b], hT[:, kc, s * 128:(s + 1) * 128], W[:, kc, 2048 + half * 512:2048 + (half + 1) * 512], start=(kc == 0), stop=(kc == DC - 1), r=["W", hk[kc]], w=[("ps", b)])
                        self.cp("act" if half else "dve", ft[:, half * 512:(half + 1) * 512], self.PS[b], r=[("ps", b)], w=[("ftok", 1, half)])
                    self.dma("pool", self.o_dk[tok0:tok0 + 128, :], ft, r=[("ftok", 1, 0), ("ftok", 1, 1)], sem="ftok1")
                self.ck("b3")
            if g.prompt:
                self.ck("b5")
        self.ck("b4")


def rope_tables(n_tok, rot_dim, grid_w=64, base=10000.0):
    rows = n_tok // grid_w
    row = np.repeat(np.arange(rows, dtype=np.float64), grid_w)
    col = np.tile(np.arange(grid_w, dtype=np.float64), rows)
    n_freq = rot_dim // 4
    inv = (np.float32(base) ** (-np.arange(n_freq, dtype=np.float32) / np.float32(n_freq))).astype(np.float64)
    ang = np.concatenate([row[:, None] * inv, col[:, None] * inv], axis=-1)
    ang = ang.astype(np.float32).astype(np.float64)
    cos, sin = np.cos(ang), np.sin(ang)
    C = np.repeat(cos, 2, axis=1).T
    S = np.repeat(sin, 2, axis=1).T.copy()
    S[0::2] *= -1.0
    return np.stack([C, S]).astype(np.float32)


def pair_swap_idx(n):
    idx = np.arange(n)
    return idx ^ 1


def prep_shared(inp, cfg):
    f = lambda a: np.ascontiguousarray(np.asarray(a, dtype=np.float32))
    sh = {}
    sh["w_mod"] = f(inp["w_mod"])
    sh["b_mod"] = f(inp["b_mod"])
    w = np.asarray(inp["ab_w_in"], np.float32)[0]
    q_lat, kv_lat, k_rope = w[:, 0:384], w[:, 384:640], w[:, 640:672]
    gq, gk, gv, gg, go = w[:, 672:1184], w[:, 1184:1696], w[:, 1696:2208], w[:, 2208:2240], w[:, 2240:2752]
    ggl = np.concatenate([gg[:, 0:16], gg[:, 0:16], gg[:, 16:32], gg[:, 16:32]], axis=1)
    sh["w_in"] = f(np.concatenate([q_lat, k_rope, k_rope[:, pair_swap_idx(32)], ggl, gq, gk, kv_lat, k_rope, gk, gv, go], axis=1))
    assert sh["w_in"].shape[1] == NIN
    sh["g_kv"] = f(inp["mla_g_kv"][0])
    uq = np.asarray(inp["mla_w_uq"], np.float32)[0]
    idx = np.arange(768).reshape(8, 96).copy()
    idx[:, 64:] = idx[:, 64:] ^ 1
    sh["w_uq"] = f(np.concatenate([uq, uq[:, idx.reshape(-1)]], axis=1))
    ukv = np.asarray(inp["mla_w_ukv"], np.float32)[0].reshape(256, 8, 128)
    sh["w_ukv"] = f(np.concatenate([ukv[:, :, 0:64].reshape(256, 512), ukv[:, :, 64:128].reshape(256, 512)], axis=1))
    sh["w_up"] = f(inp["gla_w_gate_up"][0])
    sh["b_up"] = f(inp["gla_b_gate"][0])
    sh["g_gla"] = f(inp["gla_g_out"][0])
    sh["w_o0"] = f(inp["ab_w_out"][0])
    qkv = np.asarray(inp["c_w_qkv"], np.float32)[0]
    q, k, v = qkv[:, 0:1024], qkv[:, 1024:2048], qkv[:, 2048:3072]
    sw = pair_swap_idx(1024)
    sh["w_qkv"] = f(np.concatenate([q, q[:, sw], k, k[:, sw], v], axis=1))
    sh["dlam"] = f(np.asarray(inp["diff_lambda"], np.float32)[0].reshape(-1))
    sh["w_o1"] = f(inp["c_w_out"][0])
    sh["w_f1"] = f(inp["w_ffn_in"])
    sh["w_f2"] = f(inp["w_ffn_out"])
    sh["ropeM"] = rope_tables(cfg.LS, 32)
    sh["ropeD"] = np.ascontiguousarray(np.tile(rope_tables(cfg.LS, 64), (1, 2, 1)))
    pm = np.zeros((128, 128), np.float32)
    pm[np.arange(128) ^ 1, np.arange(128)] = 1.0
    sh["perm"] = pm
    return sh


def core_inputs(inp, cfg, core, sh):
    f = lambda a: np.ascontiguousarray(np.asarray(a, dtype=np.float32))
    m = dict(sh)
    m["x_s"] = f(inp["x_sample"][core])
    m["x_p"] = f(np.asarray(inp["x_prompt"])[cfg.NPS * core:cfg.NPS * (core + 1)].reshape(cfg.NPS * cfg.LP, D))
    m["ckv_c"] = f(inp["cache_mla_ckv"][core, 0])
    m["kr_c"] = f(inp["cache_mla_krope"][core, 0])
    m["sf_in"] = f(inp["state_gla_fwd"][core, 0])
    m["sb_in"] = f(inp["state_gla_bwd"][core, 0])
    m["dk_c"] = f(np.asarray(inp["cache_diff_k"])[core, 0].reshape(cfg.CTX, D))
    m["dv_c"] = f(np.asarray(inp["cache_diff_v"])[core, 0].reshape(cfg.CTX, D))
    sv = np.zeros((84, 128), np.float32)
    sv[0:64] = np.asarray(inp["g_norm"], np.float32).reshape(64, 128)
    sv[64:67] = np.asarray(inp["mla_g_q"], np.float32)[0].reshape(3, 128)
    sv[67] = np.asarray(inp["diff_g_out"], np.float32)[0]
    sv[68:76] = np.asarray(inp["c"], np.float32)[core].reshape(8, 128)
    sv[76:84] = np.asarray(inp["c_ctx"], np.float32).reshape(8, 128)
    m["smallv"] = sv
    return m


_NC_CACHE = {}


def run(inp, cfg, n_cores, stop_after=None, trace=False):
    key = (cfg.LS, cfg.CTX, cfg.LP, cfg.NPS, stop_after)
    if key not in _NC_CACHE:
        _NC_CACHE[key] = K(cfg, stop_after).build()
    nc = _NC_CACHE[key]
    sh = prep_shared(inp, cfg)
    in_maps = [core_inputs(inp, cfg, c, sh) for c in range(n_cores)]
    res = run_bass_kernel_spmd(nc, in_maps, core_ids=list(range(n_cores)), **({"trace": True} if trace else {}))
    R = res.results
    NPS, LP = cfg.NPS, cfg.LP
    cat = lambda k: np.concatenate([np.asarray(r[k]) for r in R], axis=0)
    y_p = cat("y_p").reshape(n_cores * NPS, LP, D)
    y_s = np.stack([np.asarray(r["y_s"]) for r in R])
    ckv = cat("o_ckv").reshape(n_cores * NPS, 1, LP, 256)
    kr = cat("o_kr").reshape(n_cores * NPS, 1, LP, 32)
    sf = cat("o_sf").reshape(n_cores * NPS, 1, 4, 128, 128)
    sb = cat("o_sb").reshape(n_cores * NPS, 1, 4, 128, 128)
    dk = cat("o_dk").reshape(n_cores * NPS, 1, LP, 8, 2, 64)
    dv = cat("o_dv").reshape(n_cores * NPS, 1, LP, 8, 128)
    outs = tuple(np.ascontiguousarray(a.astype(np.float32)) for a in (y_p, y_s, ckv, kr, sf, sb, dk, dv))
    return outs, res


def kernel(**inputs):
    cfg = Cfg()
    outs, _ = run(inputs, cfg, 8)
    return outs
```

```python
import contextlib
import math
import numpy as np
import concourse.bass as bass
import concourse.mybir as mybir
from concourse.bass_utils import run_bass_kernel_spmd

F32 = mybir.dt.float32
BF16 = mybir.dt.bfloat16
ALU = mybir.AluOpType
AF = mybir.ActivationFunctionType
AX = mybir.AxisListType

ENGS = ("pe", "act", "dve", "pool", "sp")
D = 1024
DC = 8
TT = 512
EPS = 1e-6
FH = 2816
FC = 22
NIN = 3360


class Op:
    __slots__ = ("eng", "fn", "deps", "dma", "sig", "val")

    def __init__(self, eng, fn, dma):
        self.eng = eng
        self.fn = fn
        self.dma = dma
        self.deps = ()
        self.sig = False
        self.val = 0


class Prog:
    def __init__(self, nc):
        self.nc = nc
        self.ops = {e: [] for e in ENGS}
        self.lastw = {}
        self.readers = {}
        self.dma_cnt = {}
        self.dma_rr = {e: 0 for e in ENGS}
        self.dma_last = {}
        self.nsem = {"sp": 44, "pool": 44, "act": 6, "pe": 1, "dve": 1}

    def op(self, eng, fn, reads=(), writes=(), dma=None):
        if dma is not None:
            dma = (eng, self.dma_rr[eng] % self.nsem[eng])
            self.dma_rr[eng] += 1
        psr = [k for k in reads if isinstance(k, tuple) and k[0] == "ps"]
        if psr:
            reads = [k for k in reads if k not in psr]
            writes = list(writes) + psr
        o = Op(eng, fn, dma)
        deps = {}
        if dma is not None:
            prev = self.dma_last.get(dma)
            if prev is not None:
                deps[id(prev)] = prev
            self.dma_last[dma] = o
        for k in reads:
            w = self.lastw.get(k)
            if w is not None:
                deps[id(w)] = w
        for k in writes:
            w = self.lastw.get(k)
            if w is not None:
                deps[id(w)] = w
            for r in self.readers.get(k, ()):
                deps[id(r)] = r
        dl = []
        for d in deps.values():
            if d.eng == "pe" and eng == "pe" and d.dma is None and dma is None:
                continue
            dl.append(d)
        o.deps = dl
        for k in reads:
            lst = self.readers.setdefault(k, [])
            if dma is None:
                lst[:] = [r for r in lst if not (r.eng == eng and r.dma is None)]
            lst.append(o)
        for k in writes:
            self.lastw[k] = o
            self.readers[k] = []
        if dma is not None:
            self.dma_cnt[dma] = self.dma_cnt.get(dma, 0) + 16
            o.val = self.dma_cnt[dma]
        self.ops[eng].append(o)
        return o

    def barrier(self):
        lasts = []
        for e in ENGS:
            for o in reversed(self.ops[e]):
                if o.dma is None and o.fn is not None:
                    lasts.append(o)
                    break
        seen = {}
        for e in ENGS:
            for o in self.ops[e]:
                if o.dma is not None:
                    seen[o.dma] = o
        lasts.extend(seen.values())
        for e in ENGS:
            o = Op(e, None, None)
            o.deps = list(lasts)
            self.ops[e].append(o)
        self.lastw = {}
        self.readers = {}

    def check_deadlock(self):
        sem = {}
        pc = {e: 0 for e in ENGS}
        total = sum(len(v) for v in self.ops.values())
        done = 0
        while done < total:
            progressed = False
            for e in ENGS:
                while pc[e] < len(self.ops[e]):
                    o = self.ops[e][pc[e]]
                    ok = True
                    for d in o.deps:
                        k = ("d", d.dma) if d.dma is not None else ("e", d.eng)
                        assert d.val > 0, ("dep without signal value", e, pc[e])
                        if sem.get(k, 0) < d.val:
                            ok = False
                            break
                    if not ok:
                        break
                    if o.fn is not None:
                        if o.dma is not None:
                            k = ("d", o.dma)
                            sem[k] = sem.get(k, 0) + 16
                            assert sem[k] == o.val, ("dma sem order", k, sem[k], o.val)
                        elif o.sig:
                            k = ("e", e)
                            sem[k] = sem.get(k, 0) + 1
                            assert sem[k] == o.val
                    pc[e] += 1
                    done += 1
                    progressed = True
            if not progressed:
                raise RuntimeError("DEADLOCK in semaphore protocol at " + str(pc))
        print("[prog] ops per engine:", {e: len(self.ops[e]) for e in ENGS})

    def emit(self):
        nc = self.nc
        for e in ENGS:
            for o in self.ops[e]:
                for d in o.deps:
                    d.sig = True
        with contextlib.ExitStack() as st:
            esem = {e: st.enter_context(nc.semaphore("s_" + e)) for e in ENGS}
            dsem = {k: st.enter_context(nc.semaphore("d_%s%d" % k)) for k in self.dma_cnt}
            for e in ENGS:
                c = 0
                for o in self.ops[e]:
                    if o.dma is None and o.sig:
                        c += 1
                        o.val = c
            self.check_deadlock()
            block = st.enter_context(nc.Block())

            def body(ename):
                def f(eng):
                    waited = {}
                    for o in self.ops[ename]:
                        need = {}
                        for d in o.deps:
                            s = dsem[d.dma] if d.dma is not None else esem[d.eng]
                            key = id(s)
                            if key not in need or need[key][1] < d.val:
                                need[key] = (s, d.val)
                        for key, (s, v) in need.items():
                            if waited.get(key, 0) >= v:
                                continue
                            eng.wait_ge(s, v)
                            waited[key] = v
                        if o.fn is None:
                            continue
                        ins = o.fn(eng)
                        if o.dma is not None:
                            ins.then_inc(dsem[o.dma], 16)
                        elif o.sig:
                            ins.then_inc(esem[ename], 1)

                return f

            block.tensor(body("pe"))
            block.scalar(body("act"))
            block.vector(body("dve"))
            block.gpsimd(body("pool"))
            block.sync(body("sp"))


class Arena:
    def __init__(self, nc, nbytes=212000):
        self.n4 = nbytes // 4
        self.t = nc.alloc_sbuf_tensor("arena", [128, self.n4], F32)
        self.off = 0

    def alloc(self, shape, dt):
        esz = 4 if dt == F32 else 2
        n = int(np.prod(shape[1:]))
        nb = (n * esz + 31) // 32 * 32
        assert self.off + nb <= self.n4 * 4, ("SBUF arena overflow", self.off, nb)
        o4 = self.off // 4
        v = self.t[:, o4:o4 + nb // 4]
        if dt != F32:
            v = v.bitcast(dt)
        v = v[:, 0:n]
        if len(shape) > 2:
            names = " ".join(f"d{i}" for i in range(1, len(shape)))
            kw = {f"d{i}": shape[i] for i in range(1, len(shape))}
            v = v.rearrange(f"p ({names}) -> p {names}", **kw)
        if shape[0] < 128:
            v = v[0:shape[0]]
        self.off += nb
        return v


def bc_mid(ap, n):
    a = ap.ap
    return bass.AP(ap.tensor, ap.offset, [list(a[0]), [0, n]] + [list(x) for x in a[1:]])


def bc_part(ap1d, nparts):
    return bass.AP(ap1d.tensor, ap1d.offset, [[0, nparts]] + [list(x) for x in ap1d.ap])


class Cfg:
    def __init__(self, LS=4096, CTX=512, LP=256, NPS=2):
        self.LS, self.CTX, self.LP, self.NPS = LS, CTX, LP, NPS


class Grp:
    pass


class StopBuild(Exception):
    pass


class K:
    def __init__(self, cfg, stop_after=None):
        self.cfg = cfg
        self.stop_after = stop_after
        nc = self.nc = bass.Bass("TRN2", target_bir_lowering=False)
        self.P = Prog(nc)
        self.AR = Arena(nc)
        self.PSP = [nc.alloc_psum_tensor(f"pp{i}", [128, 1024], F32)[:] for i in range(4)]
        self.PS = [self.PSP[i // 2][:, (i % 2) * 512:(i % 2 + 1) * 512] for i in range(8)]
        self.pi = 0
        self.pool_banks = list(range(8))
        self.uid = 0
        self.io()
        self.groups()

    def din(self, name, shape, dt=F32):
        return self.nc.dram_tensor(name, list(shape), dt, kind="ExternalInput").ap()

    def dout(self, name, shape):
        return self.nc.dram_tensor(name, list(shape), F32, kind="ExternalOutput").ap()

    def dscr(self, name, shape, dt):
        return self.nc.dram_tensor(name, list(shape), dt, kind="Internal").ap()

    def io(self):
        c = self.cfg
        NPT = c.NPS * c.LP
        self.x_s = self.din("x_s", [c.LS, D])
        self.x_p = self.din("x_p", [NPT, D])
        self.ckv_c = self.din("ckv_c", [c.CTX, 256])
        self.kr_c = self.din("kr_c", [c.CTX, 32])
        self.sf_in = self.din("sf_in", [4, 128, 128])
        self.sb_in = self.din("sb_in", [4, 128, 128])
        self.dk_c = self.din("dk_c", [c.CTX, D])
        self.dv_c = self.din("dv_c", [c.CTX, D])
        self.smallv = self.din("smallv", [84, 128])
        self.w_mod = self.din("w_mod", [2, D, 6144])
        self.b_mod = self.din("b_mod", [2, 6144])
        self.w_in = self.din("w_in", [D, NIN])
        self.g_kv = self.din("g_kv", [256])
        self.w_uq = self.din("w_uq", [384, 1536])
        self.w_ukv = self.din("w_ukv", [256, 1024])
        self.w_up = self.din("w_up", [2, 16, 512])
        self.b_up = self.din("b_up", [2, 512])
        self.g_gla = self.din("g_gla", [128])
        self.w_o0 = self.din("w_o0", [D, D])
        self.w_qkv = self.din("w_qkv", [D, 5120])
        self.dlam = self.din("dlam", [256])
        self.w_o1 = self.din("w_o1", [D, D])
        self.w_f1 = self.din("w_f1", [2, D, 2 * FH])
        self.w_f2 = self.din("w_f2", [2, FH, D])
        self.ropeM = self.din("ropeM", [2, 32, c.LS])
        self.ropeD = self.din("ropeD", [2, 128, c.LS])
        self.perm_in = self.din("perm", [128, 128])
        self.y_p = self.dout("y_p", [NPT, D])
        self.y_s = self.dout("y_s", [c.LS, D])
        self.o_ckv = self.dout("o_ckv", [NPT, 256])
        self.o_kr = self.dout("o_kr", [NPT, 32])
        self.o_sf = self.dout("o_sf", [c.NPS, 4, 128, 128])
        self.o_sb = self.dout("o_sb", [c.NPS, 4, 128, 128])
        self.o_dk = self.dout("o_dk", [NPT, D])
        self.o_dv = self.dout("o_dv", [NPT, D])
        self.wb_in = self.dscr("wb_in", [128, 8, NIN], BF16)
        self.wb_uq = self.dscr("wb_uq", [128, 3, 1536], BF16)
        self.wb_ukv = self.dscr("wb_ukv", [128, 2, 1024], BF16)
        self.wb_o0 = self.dscr("wb_o0", [128, 8, D], BF16)
        self.wb_qkv = self.dscr("wb_qkv", [128, 8, 5120], BF16)
        self.wb_o1 = self.dscr("wb_o1", [128, 8, D], BF16)
        self.wb_f1 = [self.dscr(f"wb_f1_{l}", [128, 8, 2 * FH], BF16) for l in range(2)]
        self.wb_f2 = [self.dscr(f"wb_f2_{l}", [128, FC, D], BF16) for l in range(2)]

    def groups(self):
        c = self.cfg
        gs = Grp()
        gs.name, gs.gi, gs.ntok, gs.ctx, gs.rope, gs.prompt = "s", 0, c.LS, c.CTX, True, False
        gs.x_in, gs.y_out = self.x_s, self.y_s
        gs.seqs = [(0, c.LS)]
        gs.att = [(q0, TT, 0, c.CTX + c.LS) for q0 in range(0, c.LS, TT)]
        gp = Grp()
        gp.name, gp.gi, gp.ntok, gp.ctx, gp.rope, gp.prompt = "p", 1, c.NPS * c.LP, 0, False, True
        gp.x_in, gp.y_out = self.x_p, self.y_p
        gp.seqs = [(i * c.LP, c.LP) for i in range(c.NPS)]
        gp.att = [(i * c.LP, c.LP, i * c.LP, c.LP) for i in range(c.NPS)]
        for g in (gs, gp):
            assert g.ntok % TT == 0
            g.nt = g.ntok // TT
            g.nk = g.ctx + g.ntok
            n = g.name
            g.xT = self.dscr(f"xT_{n}", [g.nt, 128, DC, TT], F32)
            g.mixT = self.dscr(f"mixT_{n}", [g.nt, 128, DC, TT], BF16)
            g.Kf = self.dscr(f"Kf_{n}", [8, 128, g.nk], BF16)
            g.Qf = self.dscr(f"Qf_{n}", [8, 128, g.ntok], BF16)
            g.Vf = self.dscr(f"Vf_{n}", [8, 128, g.nk // 128, 128], BF16)
            g.gqT = self.dscr(f"gqT_{n}", [4, 128, g.ntok], BF16)
            g.gkT = self.dscr(f"gkT_{n}", [4, 128, g.ntok], BF16)
            g.gk_tok = self.dscr(f"gkt_{n}", [g.ntok // 128, 128, 512], BF16)
            g.gv_tok = self.dscr(f"gvt_{n}", [g.ntok // 128, 128, 512], BF16)
            g.go_tok = self.dscr(f"got_{n}", [g.ntok // 128, 128, 512], BF16)
            g.lr = self.dscr(f"lr_{n}", [64, g.ntok], F32)
            g.of = self.dscr(f"of_{n}", [g.ntok // 128, 128, 512], F32)
        self.G = [gs, gp]

    def bank(self):
        i = self.pool_banks[self.pi % len(self.pool_banks)]
        self.pi += 1
        return i

    def mm(self, out, lhsT, rhs, start=True, stop=True, r=(), w=()):
        self.P.op("pe", lambda e: e.matmul(out, lhsT, rhs, start=start, stop=stop), reads=r, writes=w)

    def tr(self, out, in_, ident, r=(), w=()):
        self.P.op("pe", lambda e: e.transpose(out, in_, ident), reads=r, writes=w)

    def act(self, out, in_, func, r=(), w=(), bias=None, scale=None, accum=None):
        kw = {}
        if bias is not None:
            kw["bias"] = bias
        if scale is not None:
            kw["scale"] = scale
        if accum is not None:
            kw["accum_out"] = accum
        self.P.op("act", lambda e: e.activation(out, in_, func, **kw), reads=r, writes=w)

    def tt(self, eng, out, in0, in1, op, r=(), w=()):
        self.P.op(eng, lambda e: e.tensor_tensor(out, in0, in1, op), reads=r, writes=w)

    def ts(self, eng, out, in0, s1, s2, op0, op1=None, r=(), w=()):
        if op1 is None:
            self.P.op(eng, lambda e: e.tensor_scalar(out, in0, s1, None, op0), reads=r, writes=w)
        else:
            self.P.op(eng, lambda e: e.tensor_scalar(out, in0, s1, s2, op0, op1), reads=r, writes=w)

    def stt(self, eng, out, in0, scalar, in1, op0, op1, r=(), w=()):
        self.P.op(eng, lambda e: e.scalar_tensor_tensor(out, in0, scalar, in1, op0, op1), reads=r, writes=w)

    def cp(self, eng, out, in_, r=(), w=()):
        if eng == "act":
            self.P.op("act", lambda e: e.activation(out, in_, AF.Copy), reads=r, writes=w)
        else:
            self.P.op(eng, lambda e: e.tensor_copy(out, in_), reads=r, writes=w)

    def recip(self, out, in_, r=(), w=()):
        self.P.op("dve", lambda e: e.reciprocal(out, in_), reads=r, writes=w)

    def memset(self, eng, ap, val, w=()):
        self.P.op(eng, lambda e: e.memset(ap, val), writes=w)

    def dma(self, q, out, in_, r=(), w=(), sem=None, accum=False):
        assert sem is not None
        if accum:
            self.P.op(q, lambda e: e.dma_start(out=out, in_=in_, accum_op=ALU.add), reads=r, writes=w, dma=sem)
        else:
            self.P.op(q, lambda e: e.dma_start(out=out, in_=in_), reads=r, writes=w, dma=sem)

    def rstd_from(self, out_sb, ss_ap, n, r, w):
        self.act(out_sb, ss_ap, AF.Ln, r=r, w=w, scale=1.0 / n, bias=self.eps_t[0:out_sb.shape[0], :])
        self.act(out_sb, out_sb, AF.Exp, r=w, w=w, scale=-0.5)

    def ck(self, name):
        if self.stop_after == name:
            raise StopBuild()

    def build(self):
        try:
            return self.build_()
        except StopBuild:
            return self.finish()

    def build_(self):
        self.setup()
        if self.stop_after in ("setup", "s0", "s1", "s2", "s3", "s4", "s5"):
            return self.finish()
        for g in self.G:
            self.phaseA0(g)
        if self.stop_after == "A0":
            return self.finish()
        for g in self.G:
            self.attn_phase(g, layer=0)
        if self.stop_after == "B0":
            return self.finish()
        for g in self.G:
            self.gla_phase(g)
        if self.stop_after == "C0":
            return self.finish()
        self.phaseD(0)
        if self.stop_after == "D0":
            return self.finish()
        for g in self.G:
            self.phaseA1(g)
        if self.stop_after == "A1":
            return self.finish()
        for g in self.G:
            self.attn_phase(g, layer=1)
        self.phaseD(1)
        return self.finish()

    def finish(self):
        self.P.barrier()
        self.P.emit()
        return self.nc

    def phase_begin(self):
        self.P.barrier()
        self.AR.off = self.persist_off
        self.pool_banks = list(range(8))

    def setup(self):
        A = self.AR
        P = self.P
        self.ident_f = A.alloc([128, 128], F32)
        self.ident_b = A.alloc([128, 128], BF16)
        self.ones_b = A.alloc([128, 128], BF16)
        self.ones_f = A.alloc([128, 128], F32)
        self.triLE = A.alloc([128, 128], F32)
        self.triGE = A.alloc([128, 128], F32)
        self.triLT = A.alloc([128, 128], F32)
        self.triGT = A.alloc([128, 128], F32)
        self.eps_t = A.alloc([128, 1], F32)
        self.pp = A.alloc([128, 84], F32)
        self.scd = A.alloc([128, 8, 2], F32)
        self.modpp = A.alloc([128, 2, 48, 2], F32)
        self.sc = A.alloc([128, 2, 2, 4, 8], F32)
        self.g_kv_bc = A.alloc([128, 256], F32)
        self.g_gla_bc = A.alloc([128, 128], F32)
        self.lam_t = A.alloc([128, 8], F32)
        self.wup = A.alloc([64, 512], F32)
        self.permB = A.alloc([128, 128], BF16)
        self.persist_off = A.off

        self.memset("pool", self.ident_f, 0.0, w=["ident_f"])
        P.op("pool", lambda e: e.affine_select(self.ident_f, self.ident_f, [[-1, 128]], ALU.not_equal, 1.0, base=0, channel_multiplier=1), reads=["ident_f"], writes=["ident_f"])
        self.cp("pool", self.ident_b, self.ident_f, r=["ident_f"], w=["ident_b"])
        self.memset("pool", self.ones_b, 1.0, w=["ones_b"])
        self.memset("pool", self.ones_f, 1.0, w=["ones_f"])
        self.memset("pool", self.eps_t, EPS, w=["eps"])
        for nm, t, pat, cm, cmp_ in (("triLE", self.triLE, 1, -1, ALU.is_ge), ("triGE", self.triGE, -1, 1, ALU.is_ge),
                                     ("triLT", self.triLT, 1, -1, ALU.is_gt), ("triGT", self.triGT, -1, 1, ALU.is_gt)):
            self.memset("pool", t, 1.0, w=[nm])
            P.op("pool", (lambda t, pat, cm, cmp_: (lambda e: e.affine_select(t, t, [[pat, 128]], cmp_, 0.0, base=0, channel_multiplier=cm)))(t, pat, cm, cmp_), reads=[nm], writes=[nm])

        if self.stop_after == "s0":
            return
        def cast(dst, src, kc, n, key):
            sv = src.rearrange("(kc p) n -> p kc n", p=128)
            step = 1408 if n > 1408 else n
            for i in range(0, n, step):
                j = min(n, i + step)
                self.dma("pool", dst[:, :, i:j], sv[:, :, i:j], w=[key + str(i)], sem="cast")
        cast(self.wb_in, self.w_in, 8, NIN, "wb_in")
        cast(self.wb_uq, self.w_uq, 3, 1536, "wb_uq")
        cast(self.wb_ukv, self.w_ukv, 2, 1024, "wb_ukv")
        cast(self.wb_o0, self.w_o0, 8, D, "wb_o0")
        self.cast = cast

        if self.stop_after == "s1":
            return
        pstg = A.alloc([128, 128], F32)
        self.dma("sp", pstg, self.perm_in, w=["pstg"], sem="x")
        self.cp("dve", self.permB, pstg, r=["pstg"], w=["permB"])
        stg = A.alloc([128, 128], F32)
        self.dma("sp", stg[0:84, :], self.smallv, w=["stg"], sem="stg")
        b = self.bank()
        self.tr(self.PS[b][:, 0:84], stg[0:84, :], self.ident_f[0:84, 0:84], r=["stg", "ident_f"], w=[("ps", b)])
        self.cp("dve", self.pp, self.PS[b][:, 0:84], r=[("ps", b)], w=["pp"])
        for g in range(2):
            self.act(self.scd[:, :, g], self.pp[:, 68 + g * 8:76 + g * 8], AF.Silu, r=["pp"], w=[("scd", g)])
        self.dma("sp", self.g_kv_bc, bc_part(self.g_kv, 128), w=["g_kv_bc"], sem="c1")
        self.dma("sp", self.g_gla_bc, bc_part(self.g_gla, 128), w=["g_gla_bc"], sem="c2")
        self.memset("pool", self.wup, 0.0, w=["wup"])
        self.dma("sp", self.wup[0:16, :], self.w_up[0], r=["wup"], w=["wup0"], sem="c3")
        self.dma("sp", self.wup[16:17, :], self.b_up[0:1, :], r=["wup"], w=["wup1"], sem="c4")
        self.dma("sp", self.wup[32:48, :], self.w_up[1], r=["wup"], w=["wup2"], sem="c5")
        self.dma("sp", self.wup[48:49, :], self.b_up[1:2, :], r=["wup"], w=["wup3"], sem="c6")

        if self.stop_after == "s2":
            return
        dl = A.alloc([128, 256], F32)
        dl2 = A.alloc([128, 128], F32)
        self.dma("sp", dl, bc_part(self.dlam, 128), w=["dl"], sem="c7")
        dlv = dl.rearrange("p (a b d) -> p a b d", a=2, b=2)
        self.tt("dve", dl2.rearrange("p (a d) -> p a d", a=2), dlv[:, :, 0, :], dlv[:, :, 1, :], ALU.mult, r=["dl"], w=["dl2"])
        lam2 = A.alloc([128, 2], F32)
        P.op("dve", lambda e: e.tensor_reduce(lam2, dl2.rearrange("p (a d) -> p a d", a=2), AX.X, ALU.add), reads=["dl2"], writes=["lam2"])
        self.act(lam2, lam2, AF.Exp, r=["lam2"], w=["lam2"])
        lam_init = 0.8 - 0.6 * math.exp(-0.3 * 1)
        self.stt("dve", self.lam_t[:, 0:1], lam2[:, 1:2], -lam_init, lam2[:, 0:1], ALU.add, ALU.subtract, r=["lam2"], w=["lam_t0"])
        self.ts("dve", self.lam_t[:, 1:2], self.pp[:, 67:68], 1.0 - lam_init, None, ALU.mult, r=["pp"], w=["lam_t1"])

        if self.stop_after == "s3":
            return
        mod_sb = A.alloc([2, 6144], F32)
        bmod_sb = A.alloc([2, 6144], F32)
        NWS = 5
        wslab = [A.alloc([128, 8, 512], F32) for _ in range(NWS)]
        si = 0
        for l in range(2):
            self.dma("sp", bmod_sb, bc_part(self.b_mod[l], 2), r=[], w=["bmod"], sem="bmod")
            wv = self.w_mod[l].rearrange("(kc p) n -> p kc n", p=128)
            for n in range(12):
                s = si % NWS
                si += 1
                self.dma("sp", wslab[s], wv[:, :, n * 512:(n + 1) * 512], w=[("wslab", s)], sem=f"wslab{s}")
                b = self.bank()
                for kc in range(8):
                    self.mm(self.PS[b][0:2, :], self.scd[:, kc, :], wslab[s][:, kc, :], start=(kc == 0), stop=(kc == 7),
                            r=[("wslab", s), ("scd", 0), ("scd", 1)], w=[("ps", b)])
                self.tt("dve", mod_sb[:, n * 512:(n + 1) * 512], self.PS[b][0:2, :], bmod_sb[:, n * 512:(n + 1) * 512], ALU.add,
                        r=[("ps", b), "bmod"], w=[("mod_sb", n)])
            if self.stop_after == "s4":
                continue
            b = self.bank()
            for j in range(48):
                self.tr(self.PS[b][:, 2 * j:2 * j + 2], mod_sb[:, j * 128:(j + 1) * 128], self.ident_f[0:2, 0:2],
                        r=[("mod_sb", j // 4), "ident_f"], w=[("ps", b)])
            self.cp("dve", self.modpp[:, l].rearrange("p j g -> p (j g)"), self.PS[b][:, 0:96], r=[("ps", b)], w=[("modpp", l)])
            if self.stop_after == "s5":
                continue
            for g in range(2):
                mp = self.modpp[:, l, :, g]
                gn = lambda f: self.pp[:, (l * 4 + f) * 8:(l * 4 + f) * 8 + 8]
                self.stt("dve", self.sc[:, l, g, 0, :], mp[:, 8:16], 1.0, gn(0), ALU.add, ALU.mult, r=[("modpp", l), "pp"], w=[("sc", l, g, 0)])
                self.tt("dve", self.sc[:, l, g, 1, :], mp[:, 16:24], gn(1), ALU.mult, r=[("modpp", l), "pp"], w=[("sc", l, g, 1)])
                self.stt("dve", self.sc[:, l, g, 2, :], mp[:, 32:40], 1.0, gn(2), ALU.add, ALU.mult, r=[("modpp", l), "pp"], w=[("sc", l, g, 2)])
                self.tt("dve", self.sc[:, l, g, 3, :], mp[:, 40:48], gn(3), ALU.mult, r=[("modpp", l), "pp"], w=[("sc", l, g, 3)])

    def sc_a(self, l, g, kind, c):
        return self.sc[:, l, g, kind, c:c + 1]

    def sc_shift(self, l, g, which, c):
        return self.modpp[:, l, which * 8 + c, g:g + 1]

    def load_xT_from_input(self, g, t, xT):
        A = self.AR
        xt = self.x_tok
        self.dma("sp", xt, g.x_in[t * TT:(t + 1) * TT, :].rearrange("(s p) d -> p s d", p=128), w=["x_tok"], sem="x_tok")
        for c in range(DC):
            b = self.bank()
            for s in range(4):
                self.tr(self.PS[b][:, s * 128:(s + 1) * 128], xt[:, s, c * 128:(c + 1) * 128], self.ident_f, r=["x_tok", "ident_f"], w=[("ps", b)])
            self.cp("act" if c % 2 else "dve", xT[:, c, :], self.PS[b], r=[("ps", b)], w=[("xT", c)])
        self.dma("pool", g.xT[t], xT, r=[("xT", c) for c in range(DC)], w=[("xTs", g.name, t)], sem="xT_st")

    def norm_mod(self, xT, hT, l, g, which, sq, rstd, tmp, xkeys, hkeys=None):
        if hkeys is None:
            hkeys = [("hT", c) for c in range(DC)]
        kind = 0 if which == 0 else 2
        shw = 0 if which == 0 else 3
        for c in range(DC):
            self.act(sq[:, c, :], xT[:, c, :], AF.Square, r=[xkeys[c]], w=[("sq", c)])
        b = self.bank()
        for c in range(DC):
            self.mm(self.PS[b], self.ones_b, sq[:, c, :], start=(c == 0), stop=(c == DC - 1), r=[("sq", c), "ones_b"], w=[("ps", b)])
        self.rstd_from(rstd, self.PS[b], D, r=[("ps", b), "eps"], w=["rstd"])
        for c in range(DC):
            self.stt("dve", tmp[:, c, :], xT[:, c, :], self.sc_a(l, g.gi, kind, c), rstd, ALU.mult, ALU.mult,
                     r=[xkeys[c], "rstd", ("sc", l, g.gi, kind)], w=[("tmp", c)])
            self.act(hT[:, c, :], tmp[:, c, :], AF.Identity, r=[("tmp", c), ("modpp", l)], w=[hkeys[c]],
                     bias=self.sc_shift(l, g.gi, shw, c))

    def phaseA0(self, g):
        self.phase_begin()
        A = self.AR
        l = 0
        if g.gi == 0:
            self.cast(self.wb_f1[0], self.w_f1[0], 8, 2 * FH, "wb_f10")
            self.cast(self.wb_f2[0], self.w_f2[0], FC, D, "wb_f20")
        W = A.alloc([128, 8, NIN], BF16)
        Wuq = A.alloc([128, 3, 1536], BF16)
        Wukv = A.alloc([128, 2, 1024], BF16)
        self.dma("sp", W, self.wb_in, r=["wb_in" + str(i) for i in range(0, NIN, 1408)], w=["W"], sem="W")
        self.dma("sp", Wuq, self.wb_uq, r=["wb_uq0", "wb_uq1408"], w=["Wuq"], sem="Wuq")
        self.dma("sp", Wukv, self.wb_ukv, r=["wb_ukv0"], w=["Wukv"], sem="Wukv")
        self.x_tok = A.alloc([128, 4, D], F32)
        xT = A.alloc([128, DC, TT], F32)
        hTs = [A.alloc([128, DC, TT], BF16) for _ in range(2)]
        sq = A.alloc([128, DC, TT], BF16)
        tmp = A.alloc([128, DC, TT], F32)
        rstd = A.alloc([128, TT], F32)
        qlat = A.alloc([128, 3, TT], F32)
        qn = A.alloc([128, 3, TT], BF16)
        QT = A.alloc([96, 8, TT], BF16)
        rtmp = A.alloc([96, 2, TT], F32)
        krT = A.alloc([32, TT], BF16)
        lrT = A.alloc([64, TT], F32)
        gqs = A.alloc([128, 4, TT], BF16)
        gks = A.alloc([128, 4, TT], BF16)
        knT = A.alloc([128, 4, TT], BF16)
        ropeC = A.alloc([96, TT], F32)
        ropeS = A.alloc([96, TT], F32)
        kvns = [A.alloc([128, 288], F32) for _ in range(4)]
        sss = [A.alloc([128, 2], F32) for _ in range(4)]
        junk = A.alloc([128, 256], F32)
        ckvT = A.alloc([128, 2, TT], BF16)
        tokb = [A.alloc([128, 3, 512], BF16) for _ in range(2)]
        vst = [A.alloc([128, 8, 64], BF16) for _ in range(2)]
        self.memset("pool", lrT, 1.0, w=["lrT"])
        self.ck("a0a")
        ctx_x = None
        if g.ctx:
            ctx_x = A.alloc([128, 288], F32)

        def kv_path(ckvT_keys, k0, nkeys):
            for hp in range(4):
                b = self.bank()
                for kc in range(2):
                    self.mm(self.PS[b][:, 0:nkeys], Wukv[:, kc, hp * 128:(hp + 1) * 128], ckvT[:, kc, 0:nkeys], start=(kc == 0), stop=(kc == 1),
                            r=["Wukv"] + ckvT_keys, w=[("ps", b)])
                self.cp("act", knT[:, hp, 0:nkeys], self.PS[b][:, 0:nkeys], r=[("ps", b)], w=[("knT", hp)])
                for hh in range(2):
                    h = hp * 2 + hh
                    self.dma("pool", g.Kf[h, 0:64, k0:k0 + nkeys], knT[hh * 64:(hh + 1) * 64, hp, 0:nkeys], r=[("knT", hp)], w=[("Kf", h, k0, 0)], sem="knT_st")
            for s in range(nkeys // 128):
                b = self.bank()
                for kc in range(2):
                    self.mm(self.PS[b], ckvT[:, kc, s * 128:(s + 1) * 128], Wukv[:, kc, 512:1024], start=(kc == 0), stop=(kc == 1),
                            r=["Wukv"] + ckvT_keys, w=[("ps", b)])
                vs = vst[s % 2]
                self.cp("dve", vs.rearrange("p h e -> p (h e)"), self.PS[b], r=[("ps", b)], w=[("vst", s % 2)])
                kt = (k0 + s * 128) // 128
                self.dma("pool", g.Vf[:, :, kt, 0:64].rearrange("h p e -> p h e"), vs, r=[("vst", s % 2)], w=[("Vf", kt)], sem=f"vst{s % 2}")

        if g.ctx:
            for k0 in range(0, g.ctx, TT):
                nkeys = min(TT, g.ctx - k0)
                for s in range(nkeys // 128):
                    r0 = k0 + s * 128
                    self.dma("sp", ctx_x[:, 0:256], self.ckv_c[r0:r0 + 128, :], w=["ctx_x"], sem="ctx_x")
                    self.dma("sp", ctx_x[:, 256:288], self.kr_c[r0:r0 + 128, :], w=["ctx_x2"], sem="ctx_x2")
                    self.ck("a0b1")
                    b = self.bank()
                    for kc in range(2):
                        self.tr(self.PS[b][:, kc * 128:(kc + 1) * 128], ctx_x[:, kc * 128:(kc + 1) * 128], self.ident_f, r=["ctx_x", "ident_f"], w=[("ps", b)])
                    self.ck("a0b2")
                    self.tr(self.PS[b][0:32, 256:384], ctx_x[:, 256:288], self.ident_f, r=["ctx_x2", "ident_f"], w=[("ps", b)])
                    self.ck("a0b3")
                    self.cp("dve", ckvT[:, :, s * 128:(s + 1) * 128], self.PS[b][:, 0:256].rearrange("p (k t) -> p k t", k=2), r=[("ps", b)], w=[("ckvT", s)])
                    self.ck("a0b4")
                    self.cp("act", krT[:, s * 128:(s + 1) * 128], self.PS[b][0:32, 256:384], r=[("ps", b)], w=[("krT", s)])
                ns = nkeys // 128
                self.ck("a0b")
                for h in range(8):
                    self.dma("pool", g.Kf[h, 64:96, k0:k0 + nkeys], krT[:, 0:nkeys], r=[("krT", s) for s in range(ns)], w=[("Kf", h, k0, 1)], sem="krT_st")
                self.ck("a0c")
                kv_path([("ckvT", s) for s in range(ns)], k0, nkeys)

        def head(t):
            xkeys = [("xT", c) for c in range(DC)]
            self.load_xT_from_input(g, t, xT)
            self.norm_mod(xT, hTs[t % 2], l, g, 0, sq, rstd, tmp, xkeys, hkeys=[("hT", t % 2, c) for c in range(DC)])
            if g.rope:
                self.dma("sp", ropeC[64:96, :], self.ropeM[0, :, t * TT:(t + 1) * TT], w=["ropeC"], sem="ropeC")
                self.dma("sp", ropeS[64:96, :], self.ropeM[1, :, t * TT:(t + 1) * TT], w=["ropeS"], sem="ropeS")
                self.dma("sp", ropeC[0:32, :], self.ropeM[0, :, t * TT:(t + 1) * TT], w=["ropeCk"], sem="ropeCk")
                self.dma("sp", ropeS[0:32, :], self.ropeM[1, :, t * TT:(t + 1) * TT], w=["ropeSk"], sem="ropeSk")

        head(0)
        for t in range(g.nt):
            hT = hTs[t % 2]
            hk = [("hT", t % 2, c) for c in range(DC)]

            def projB(col0, m, ps_ap, b):
                for kc in range(DC):
                    self.mm(ps_ap, W[:, kc, col0:col0 + m], hT[:, kc, :], start=(kc == 0), stop=(kc == DC - 1), r=["W", hk[kc]], w=[("ps", b)])

            for c in range(3):
                b = self.bank()
                projB(c * 128, 128, self.PS[b], b)
                self.cp("act", qlat[:, c, :], self.PS[b], r=[("ps", b)], w=[("qlat", c)])
                self.tt("pool", sq[:, c, :], qlat[:, c, :], qlat[:, c, :], ALU.mult, r=[("qlat", c)], w=[("sq", c)])
            b = self.bank()
            for c in range(3):
                self.mm(self.PS[b], self.ones_b, sq[:, c, :], start=(c == 0), stop=(c == 2), r=[("sq", c), "ones_b"], w=[("ps", b)])
            self.rstd_from(rstd, self.PS[b], 384, r=[("ps", b), "eps"], w=["rstd"])
            for c in range(3):
                self.stt("dve", qn[:, c, :], qlat[:, c, :], self.pp[:, 64 + c:65 + c], rstd, ALU.mult, ALU.mult, r=[("qlat", c), "rstd", "pp"], w=[("qn", c)])
            self.ck("a3")
            for h in range(8):
                b = self.bank()
                for kc in range(3):
                    self.mm(self.PS[b][0:96, :], Wuq[:, kc, h * 96:(h + 1) * 96], qn[:, kc, :], start=(kc == 0), stop=(kc == 2), r=["Wuq", ("qn", kc)], w=[("ps", b)])
                if g.rope:
                    b2 = self.bank()
                    for kc in range(3):
                        self.mm(self.PS[b2][0:96, :], Wuq[:, kc, 768 + h * 96:768 + (h + 1) * 96], qn[:, kc, :], start=(kc == 0), stop=(kc == 2), r=["Wuq", ("qn", kc)], w=[("ps", b2)])
                    self.cp("act", QT[0:64, h, :], self.PS[b][0:64, :], r=[("ps", b)], w=[("QT", h, 0)])
                    self.tt("dve", rtmp[64:96, 0, :], self.PS[b][64:96, :], ropeC[64:96, :], ALU.mult, r=[("ps", b), "ropeC"], w=[("rtmp", 0)])
                    self.tt("dve", rtmp[64:96, 1, :], self.PS[b2][64:96, :], ropeS[64:96, :], ALU.mult, r=[("ps", b2), "ropeS"], w=[("rtmp", 1)])
                    self.tt("pool", QT[64:96, h, :], rtmp[64:96, 0, :], rtmp[64:96, 1, :], ALU.add, r=[("rtmp", 0), ("rtmp", 1)], w=[("QT", h, 1)])
                else:
                    self.cp("act", QT[:, h, :], self.PS[b][0:96, :], r=[("ps", b)], w=[("QT", h, 0), ("QT", h, 1)])
            self.dma("pool", g.Qf[:, 0:96, t * TT:(t + 1) * TT].rearrange("h p t -> p h t"), QT,
                     r=[("QT", h, i) for h in range(8) for i in range(2)], w=[("Qf", t)], sem="QT_st")
            self.ck("a4")
            b = self.bank()
            projB(384, 32, self.PS[b][0:32, :], b)
            if g.rope:
                b2 = self.bank()
                projB(416, 32, self.PS[b2][0:32, :], b2)
                self.tt("dve", rtmp[0:32, 0, :], self.PS[b][0:32, :], ropeC[0:32, :], ALU.mult, r=[("ps", b), "ropeCk"], w=[("rtmpk", 0)])
                self.tt("dve", rtmp[0:32, 1, :], self.PS[b2][0:32, :], ropeS[0:32, :], ALU.mult, r=[("ps", b2), "ropeSk"], w=[("rtmpk", 1)])
                self.tt("pool", krT, rtmp[0:32, 0, :], rtmp[0:32, 1, :], ALU.add, r=[("rtmpk", 0), ("rtmpk", 1)], w=[("krT", 0)])
            else:
                self.cp("dve", krT, self.PS[b][0:32, :], r=[("ps", b)], w=[("krT", 0)])
            k0 = g.ctx + t * TT
            for h in range(8):
                self.dma("pool", g.Kf[h, 64:96, k0:k0 + TT], krT, r=[("krT", 0)], w=[("Kf", h, k0, 1)], sem="krT_st")
            b = self.bank()
            projB(448, 64, self.PS[b][0:64, :], b)
            self.cp("dve", lrT[0:16, :], self.PS[b][0:16, :], r=[("ps", b), "lrT"], w=[("lrT", 0)])
            self.cp("dve", lrT[32:48, :], self.PS[b][32:48, :], r=[("ps", b), "lrT"], w=[("lrT", 1)])
            self.dma("pool", g.lr[:, t * TT:(t + 1) * TT], lrT, r=[("lrT", 0), ("lrT", 1), "lrT"], w=[("lr", t)], sem="lrT_st")
            self.ck("a5")
            for h in range(4):
                b = self.bank()
                projB(512 + h * 128, 128, self.PS[b], b)
                self.act(gqs[:, h, :], self.PS[b], AF.Copy, r=[("ps", b)], w=[("gqs", h)], scale=128.0 ** -0.5)
            self.dma("pool", g.gqT[:, :, t * TT:(t + 1) * TT].rearrange("h p t -> p h t"), gqs, r=[("gqs", h) for h in range(4)], w=[("gqT", t)], sem="gqs_st")
            for h in range(4):
                b = self.bank()
                projB(1024 + h * 128, 128, self.PS[b], b)
                self.cp("dve", gks[:, h, :], self.PS[b], r=[("ps", b)], w=[("gks", h)])
            self.dma("pool", g.gkT[:, :, t * TT:(t + 1) * TT].rearrange("h p t -> p h t"), gks, r=[("gks", h) for h in range(4)], w=[("gkT", t)], sem="gks_st")
            if t + 1 < g.nt:
                head(t + 1)
            def projA(s, col0, n, ps_ap, b):
                for kc in range(DC):
                    self.mm(ps_ap, hT[:, kc, s * 128:(s + 1) * 128], W[:, kc, col0:col0 + n], start=(kc == 0), stop=(kc == DC - 1), r=["W", hk[kc]], w=[("ps", b)])
            for s in range(4):
                tok0 = t * TT + s * 128
                b = self.bank()
                projA(s, 1536, 288, self.PS[b][:, 0:288], b)
                kv_ = kvns[s]
                ss_ = sss[s]
                self.act(junk, self.PS[b][:, 0:256], AF.Square, r=[("ps", b)], w=["junk", ("ss", s)], accum=ss_[:, 0:1])
                self.act(ss_[:, 1:2], ss_[:, 0:1], AF.Sqrt, r=[("ss", s), "eps"], w=[("ss1", s)], scale=1.0 / 256, bias=self.eps_t)
                self.recip(ss_[:, 1:2], ss_[:, 1:2], r=[("ss1", s)], w=[("ss1", s)])
                self.stt("dve", kv_[:, 0:256], self.PS[b][:, 0:256], ss_[:, 1:2], self.g_kv_bc, ALU.mult, ALU.mult, r=[("ps", b), ("ss1", s), "g_kv_bc"], w=[("kvn", s)])
                if g.prompt:
                    self.cp("act", kv_[:, 256:288], self.PS[b][:, 256:288], r=[("ps", b)], w=[("kvn2", s)])
                    self.dma("pool", self.o_ckv[tok0:tok0 + 128, :], kv_[:, 0:256], r=[("kvn", s)], sem="x")
                    self.dma("pool", self.o_kr[tok0:tok0 + 128, :], kv_[:, 256:288], r=[("kvn2", s)], sem="x")
            for s in range(4):
                tok0 = t * TT + s * 128
                ti = tok0 // 128
                tb = tokb[s % 2]
                for i in range(3):
                    b = self.bank()
                    projA(s, 1824 + i * 512, 512, self.PS[b], b)
                    if i == 2:
                        self.act(tb[:, i, :], self.PS[b], AF.Silu, r=[("ps", b)], w=[("tokb", s % 2, i)])
                    else:
                        self.cp("dve" if i == 0 else "act", tb[:, i, :], self.PS[b], r=[("ps", b)], w=[("tokb", s % 2, i)])
                self.dma("pool", g.gk_tok[ti], tb[:, 0, :], r=[("tokb", s % 2, 0)], w=[("gk_tok", ti)], sem="x")
                self.dma("pool", g.gv_tok[ti], tb[:, 1, :], r=[("tokb", s % 2, 1)], w=[("gv_tok", ti)], sem="x")
                self.dma("pool", g.go_tok[ti], tb[:, 2, :], r=[("tokb", s % 2, 2)], w=[("go_tok", ti)], sem="x")
            for s in range(4):
                b2 = self.bank()
                for kc in range(2):
                    self.tr(self.PS[b2][:, kc * 128:(kc + 1) * 128], kvns[s][:, kc * 128:(kc + 1) * 128], self.ident_f, r=[("kvn", s), "ident_f"], w=[("ps", b2)])
                self.cp("dve", ckvT[:, :, s * 128:(s + 1) * 128], self.PS[b2][:, 0:256].rearrange("p (k t) -> p k t", k=2), r=[("ps", b2)], w=[("ckvT", s)])
            self.ck("a7")
            kv_path([("ckvT", s) for s in range(4)], g.ctx + t * TT, TT)
            self.ck("a8")

    def attn_phase(self, g, layer):
        self.phase_begin()
        A = self.AR
        mla = (layer == 0)
        rows = 96 if mla else 128
        dv = 64 if mla else 128
        nm = 1 if mla else 2
        scale = (96.0 ** -0.5) if mla else (64.0 ** -0.5)
        nkt_all = g.nk // 128
        nbuf = 2
        KT = [A.alloc([128, g.nk], BF16) for _ in range(nbuf)]
        QTt = [A.alloc([128, g.ntok], BF16) for _ in range(nbuf)]
        VT = [A.alloc([128, nkt_all, 128], BF16) for _ in range(nbuf)]
        NE = 8
        E = [A.alloc([128, 2, TT], BF16) for _ in range(NE)]
        NZ = 6 if not mla else 1
        Zacc = [[A.alloc([128, 2, TT], F32) for _ in range(NZ)] for _ in range(2)]
        rz = [A.alloc([128, TT], F32) for _ in range(2)]
        o0 = A.alloc([128, TT], F32)
        o1 = A.alloc([128, TT], F32)
        osq = A.alloc([128, TT], BF16)
        orstd = A.alloc([128, TT], F32)
        mixo = [A.alloc([128, TT], BF16) for _ in range(2)]
        oraw = [[A.alloc([128, TT], F32) for _ in range(2)] for _ in range(2)]
        pending_fin = [None, None, None]
        SL = [0, 1, 2]
        OB = [6, 7]
        ZB = [4, 5]
        if mla:
            for s in range(nbuf):
                self.memset("pool", VT[s][:, :, 64:128], 1.0, w=[("VTones", s)])
        gi = 0
        mi = 0
        zi = 0
        for h in range(8):
            s = h % nbuf
            self.dma("sp", KT[s][0:rows, :], g.Kf[h, 0:rows, :], w=[("KT", s)], sem="x")
            self.dma("sp", QTt[s][0:rows, :], g.Qf[h, 0:rows, :], w=[("QT", s)], sem="x")
            self.dma("sp", VT[s][:, :, 0:dv], g.Vf[h, :, :, 0:dv], w=[("VT", s)], sem="x")
            vkeys = [("VT", s)] + ([("VTones", s)] if mla else [])
            for (q0, qn, k0, kn) in g.att:
                nkt = kn // 128
                kt0 = k0 // 128
                items = [(kt, m) for kt in range(nkt) for m in range(nm)]
                groups = [items[i:i + 2] for i in range(0, len(items), 2)]
                zpar = zi % 2
                zi += 1
                state = {"gidx": 0, "used": [False] * NZ, "cnt": [0, 0]}

                def issue_qk(grp):
                    nonlocal gi
                    sl = SL[gi % len(SL)]
                    e = gi % NE
                    gi += 1
                    pk = [("ps", 2 * sl), ("ps", 2 * sl + 1)]
                    for j, (kt, m) in enumerate(grp):
                        r0, r1 = (0, rows) if mla else (m * 64, (m + 1) * 64)
                        kk = (kt0 + kt) * 128
                        self.mm(self.PS[2 * sl + j][:, 0:qn], KT[s][r0:r1, kk:kk + 128], QTt[s][r0:r1, q0:q0 + qn], r=[("KT", s), ("QT", s)], w=pk)
                    n = len(grp)
                    src = self.PSP[sl].rearrange("p (j n) -> p j n", j=2)[:, 0:n, 0:qn]
                    self.act(E[e][:, 0:n, 0:qn], src, AF.Exp, r=pk, w=[("E", e)], scale=scale)
                    if not mla:
                        on_dma = (state["gidx"] % 2 == 1)
                        state["gidx"] += 1
                        w_ = 1 if on_dma else 0
                        if on_dma:
                            zi_ = 2 + (state["cnt"][1] % 4)
                        else:
                            zi_ = state["cnt"][0] % 2
                        state["cnt"][w_] += 1
                        zc = Zacc[zpar][zi_]
                        zk = ("Zacc", zpar, zi_)
                        firstuse = not state["used"][zi_]
                        state["used"][zi_] = True
                        if on_dma:
                            self.dma("pool", zc[:, :, 0:qn], E[e][:, :, 0:qn], r=[("E", e)], w=[zk], sem="x", accum=not firstuse)
                        elif firstuse:
                            self.cp("dve", zc[:, :, 0:qn], E[e][:, :, 0:qn], r=[("E", e)], w=[zk])
                        else:
                            self.tt("dve", zc[:, :, 0:qn], zc[:, :, 0:qn], E[e][:, :, 0:qn], ALU.add, r=[("E", e), zk], w=[zk])
                    return (e, False)

                def issue_pv(grp, eo):
                    e, on_pe = eo
                    for j, (kt, m) in enumerate(grp):
                        st_ = (kt == 0)
                        sp_ = (kt == nkt - 1)
                        M = 128
                        self.mm(self.PS[OB[m]][0:M, 0:qn], VT[s][:, kt0 + kt, 0:M], E[e][:, j, 0:qn], start=st_, stop=sp_, r=vkeys + [("E", e)], w=[("ps", OB[m])])
                    if not mla and on_pe:
                        zf = (state["zpe"] == 0)
                        state["zpe"] += 1
                        zl = (state["zpe"] == state["npe"])
                        for j, (kt, m) in enumerate(grp):
                            self.mm(self.PS[ZB[m]][:, 0:qn], self.ones_b, E[e][:, j, 0:qn], start=zf, stop=zl, r=["ones_b", ("E", e)], w=[("ps", ZB[m])])

                LA = 2
                pend = []
                for gidx_, grp in enumerate(groups):
                    pend.append((grp, issue_qk(grp)))
                    if len(pend) > LA:
                        issue_pv(*pend.pop(0))
                    for st_i, at in enumerate((3, 12, 22)):
                        if gidx_ == at and pending_fin[st_i] is not None:
                            nxt = pending_fin[st_i]()
                            pending_fin[st_i] = None
                            if st_i + 1 < 3:
                                pending_fin[st_i + 1] = nxt
                while pend:
                    issue_pv(*pend.pop(0))
                for st_i in range(3):
                    if pending_fin[st_i] is not None:
                        nxt = pending_fin[st_i]()
                        pending_fin[st_i] = None
                        if st_i + 1 < 3:
                            pending_fin[st_i + 1] = nxt
                par = mi % 2
                mi += 1
                oraw0, oraw1 = oraw[par]
                self.cp("act", oraw0[:, 0:qn], self.PS[OB[0]][:, 0:qn], r=[("ps", OB[0])], w=[("oraw", par, 0)])
                if not mla:
                    self.cp("dve", oraw1[:, 0:qn], self.PS[OB[1]][:, 0:qn], r=[("ps", OB[1])], w=[("oraw", par, 1)])
                usedz = [i for i in range(NZ) if state["used"][i]] if not mla else []

                def fin(h=h, q0=q0, qn=qn, par=par, zpar=zpar, usedz=usedz, oraw0=oraw0, oraw1=oraw1):
                    nonlocal gi
                    mo = mixo[par]
                    mk = ("mixo", par)
                    tile_i = q0 // TT
                    qo = q0 % TT
                    if mla:
                        sl = SL[gi % len(SL)]
                        gi += 1
                        pk = [("ps", 2 * sl), ("ps", 2 * sl + 1)]
                        self.mm(self.PS[2 * sl][0:64, 0:qn], self.ident_f[:, 64:128], oraw0[:, 0:qn], r=[("oraw", par, 0), "ident_f"], w=pk)
                        self.act(rz[0][0:64, 0:qn], self.PS[2 * sl][0:64, 0:qn], AF.Ln, r=pk, w=[("rz", 0)])
                        self.act(rz[0][0:64, 0:qn], rz[0][0:64, 0:qn], AF.Exp, r=[("rz", 0)], w=[("rz", 0)], scale=-1.0)
                        self.tt("dve", mo[0:64, 0:qn], oraw0[0:64, 0:qn], rz[0][0:64, 0:qn], ALU.mult, r=[("oraw", par, 0), ("rz", 0)], w=[mk])
                        pr = (h % 2) * 64
                        self.dma("pool", g.mixT[tile_i, pr:pr + 64, h // 2, qo:qo + qn], mo[0:64, 0:qn], r=[mk], w=[("mixT", tile_i, h)], sem="x")
                        return None
                    dz = [i for i in usedz if i >= 2]
                    while len(dz) > 1:
                        a_, b_ = dz[0], dz[1]
                        self.tt("dve", Zacc[zpar][a_][:, :, 0:qn], Zacc[zpar][a_][:, :, 0:qn], Zacc[zpar][b_][:, :, 0:qn], ALU.add,
                                r=[("Zacc", zpar, a_), ("Zacc", zpar, b_)], w=[("Zacc", zpar, a_)])
                        dz = dz[2:] + [a_]
                    usedz = [i for i in usedz if i < 2] + dz
                    return lambda: finB(usedz)

                def finB(usedz, h=h, q0=q0, qn=qn, par=par, zpar=zpar, oraw0=oraw0, oraw1=oraw1):
                    nonlocal gi
                    mo = mixo[par]
                    mk = ("mixo", par)
                    tile_i = q0 // TT
                    qo = q0 % TT
                    sl = SL[gi % len(SL)]
                    gi += 1
                    pk = [("ps", 2 * sl), ("ps", 2 * sl + 1)]
                    for m in range(2):
                        for ii, i in enumerate(usedz):
                            self.mm(self.PS[2 * sl + m][:, 0:qn], self.ones_f, Zacc[zpar][i][:, m, 0:qn], start=(ii == 0), stop=(ii == len(usedz) - 1),
                                    r=[("Zacc", zpar, i), "ones_f"], w=pk)
                    for m in range(2):
                        self.act(rz[m][:, 0:qn], self.PS[2 * sl + m][:, 0:qn], AF.Ln, r=pk, w=[("rz", m)])
                        self.act(rz[m][:, 0:qn], rz[m][:, 0:qn], AF.Exp, r=[("rz", m)], w=[("rz", m)], scale=-1.0)
                    self.tt("dve", o0[:, 0:qn], oraw0[:, 0:qn], rz[0][:, 0:qn], ALU.mult, r=[("oraw", par, 0), ("rz", 0)], w=["o0"])
                    self.tt("dve", o1[:, 0:qn], oraw1[:, 0:qn], rz[1][:, 0:qn], ALU.mult, r=[("oraw", par, 1), ("rz", 1)], w=["o1"])
                    self.stt("dve", o0[:, 0:qn], o1[:, 0:qn], self.lam_t[:, 0:1], o0[:, 0:qn], ALU.mult, ALU.add, r=["o0", "o1", "lam_t0"], w=["o0"])
                    self.act(osq[:, 0:qn], o0[:, 0:qn], AF.Square, r=["o0"], w=["osq"])

                    def fin2():
                        nonlocal gi
                        sl = SL[gi % len(SL)]
                        gi += 1
                        pk = [("ps", 2 * sl), ("ps", 2 * sl + 1)]
                        self.mm(self.PS[2 * sl][:, 0:qn], self.ones_b, osq[:, 0:qn], r=["osq", "ones_b"], w=pk)
                        self.rstd_from(orstd[:, 0:qn], self.PS[2 * sl][:, 0:qn], 128, r=pk + ["eps"], w=["orstd"])
                        self.stt("dve", mo[:, 0:qn], o0[:, 0:qn], self.lam_t[:, 1:2], orstd[:, 0:qn], ALU.mult, ALU.mult, r=["o0", "orstd", "lam_t1"], w=[mk])
                        self.dma("pool", g.mixT[tile_i, :, h, qo:qo + qn], mo[:, 0:qn], r=[mk], w=[("mixT", tile_i, h)], sem="x")
                    return fin2
                    sl = SL[gi % len(SL)]
                    gi += 1
                    pk = [("ps", 2 * sl), ("ps", 2 * sl + 1)]
                    self.mm(self.PS[2 * sl][:, 0:qn], self.ones_b, osq[:, 0:qn], r=["osq", "ones_b"], w=pk)
                    self.rstd_from(orstd[:, 0:qn], self.PS[2 * sl][:, 0:qn], 128, r=pk + ["eps"], w=["orstd"])
                    self.stt("dve", mo[:, 0:qn], o0[:, 0:qn], self.lam_t[:, 1:2], orstd[:, 0:qn], ALU.mult, ALU.mult, r=["o0", "orstd", "lam_t1"], w=[mk])
                    self.dma("pool", g.mixT[tile_i, :, h, qo:qo + qn], mo[:, 0:qn], r=[mk], w=[("mixT", tile_i, h)], sem="x")

                pending_fin[0] = fin
        for st_i in range(3):
            if pending_fin[st_i] is not None:
                nxt = pending_fin[st_i]()
                pending_fin[st_i] = None
                if st_i + 1 < 3:
                    pending_fin[st_i + 1] = nxt

    def gla_phase(self, g):
        self.phase_begin()
        A = self.AR
        self.pool_banks = list(range(8))
        NB = 2
        lr = A.alloc([64, g.ntok], F32)
        self.dma("sp", lr, g.lr, w=["lr"], sem="x")
        nseq = len(g.seqs)
        S = [[A.alloc([128, 4, 128], F32) for _ in range(2)] for _ in range(nseq)]
        Sb = [[A.alloc([128, 4, 128], BF16) for _ in range(2)] for _ in range(nseq)]
        ld = {}
        for nm_, shape, dt in (("qT", [128, 4, 128], BF16), ("kT", [128, 4, 128], BF16), ("kt", [128, 512], BF16), ("vt", [128, 512], BF16),
                               ("e", [128, 512], F32), ("y", [128, 512], F32), ("sufe", [128, 512], F32), ("kdec", [128, 512], BF16),
                               ("Ep", [128, 4, 128], F32), ("Em", [128, 4, 128], F32), ("qin", [128, 4, 128], BF16), ("kin", [128, 4, 128], BF16),
                               ("AT", [128, 4, 128], BF16), ("o", [128, 512], F32), ("ofl", [128, 512], F32), ("got", [128, 512], BF16)):
            ld[nm_] = [[A.alloc(shape, dt) for _ in range(NB)] for _ in range(2)]
        ssq = A.alloc([128, 4], F32)
        rs = A.alloc([128, 4], F32)
        junk = A.alloc([128, 128], F32)
        gl = A.alloc([128, 512], F32)
        glb = A.alloc([128, 512], BF16)
        glT = A.alloc([128, 4, 128], BF16)

        for si, (t0, L) in enumerate(g.seqs):
            for d in range(2):
                if g.prompt:
                    self.memset("pool", S[si][d], 0.0, w=[("S", si, d)])
                else:
                    src = self.sf_in if d == 0 else self.sb_in
                    self.dma("sp", S[si][d], src.rearrange("h k v -> k h v"), w=[("S", si, d)], sem="x")
                self.cp("act", Sb[si][d], S[si][d], r=[("S", si, d)], w=[("Sb", si, d)])
        cnt = [0, 0]
        done_tiles = set()

        def f1(d, si, i):
            t0, L = g.seqs[si]
            n = cnt[d] % NB
            cnt[d] += 1
            B = lambda nm_: ld[nm_][d][n]
            Kk = lambda nm_: (nm_, d, n)
            tok0 = t0 + i * 128
            ti = tok0 // 128
            prow = 0 if d == 0 else 32
            self.dma("sp", B("qT"), g.gqT[:, :, tok0:tok0 + 128].rearrange("h p t -> p h t"), w=[Kk("qT")], sem="x")
            self.dma("sp", B("kT"), g.gkT[:, :, tok0:tok0 + 128].rearrange("h p t -> p h t"), w=[Kk("kT")], sem="x")
            self.dma("sp", B("kt"), g.gk_tok[ti], w=[Kk("kt")], sem="x")
            self.dma("sp", B("vt"), g.gv_tok[ti], w=[Kk("vt")], sem="x")
            b = self.bank()
            self.mm(self.PS[b], lr[prow:prow + 17, tok0:tok0 + 128], self.wup[prow:prow + 17, :], r=["lr", "wup0", "wup1", "wup2", "wup3"], w=[("ps", b)])
            self.act(B("e"), self.PS[b], AF.Exp, r=[("ps", b)], w=[Kk("e")], scale=-1.0)
            self.act(B("y"), B("e"), AF.Ln, r=[Kk("e")], w=[Kk("y")], bias=1.0)
            first = (si, i) not in done_tiles
            done_tiles.add((si, i))
            return (d, si, i, n, first)

        def f2(tok):
            d, si, i, n, first = tok
            B = lambda nm_: ld[nm_][d][n]
            Kk = lambda nm_: (nm_, d, n)
            triC = self.triLE if d == 0 else self.triGE
            triS = self.triGT if d == 0 else self.triLT
            tk = ("triLE" if d == 0 else "triGE")
            tks = ("triGT" if d == 0 else "triLT")
            b = self.bank()
            self.mm(self.PS[b], triS, B("y"), r=[tks, Kk("y")], w=[("ps", b)])
            self.act(B("sufe"), self.PS[b], AF.Exp, r=[("ps", b)], w=[Kk("sufe")], scale=-1.0 / 16)
            self.tt("dve", B("kdec"), B("sufe"), B("kt"), ALU.mult, r=[Kk("sufe"), Kk("kt")], w=[Kk("kdec")])
            b = self.bank()
            for h in range(4):
                self.mm(self.PS[b][:, h * 128:(h + 1) * 128], B("y")[:, h * 128:(h + 1) * 128], triC, r=[tk, Kk("y")], w=[("ps", b)])
            pv = self.PS[b].rearrange("p (h c) -> p h c", h=4)
            self.act(B("Ep"), pv, AF.Exp, r=[("ps", b)], w=[Kk("Ep")], scale=-1.0 / 16)
            self.act(B("Em"), pv, AF.Exp, r=[("ps", b)], w=[Kk("Em")], scale=1.0 / 16)
            self.tt("dve", B("qin"), B("qT"), B("Ep"), ALU.mult, r=[Kk("qT"), Kk("Ep")], w=[Kk("qin")])
            self.tt("pool", B("kin"), B("kT"), B("Em"), ALU.mult, r=[Kk("kT"), Kk("Em")], w=[Kk("kin")])

        def f3(tok):
            d, si, i, n, first = tok
            B = lambda nm_: ld[nm_][d][n]
            Kk = lambda nm_: (nm_, d, n)
            triC = self.triLE if d == 0 else self.triGE
            tk = ("triLE" if d == 0 else "triGE")
            b = self.bank()
            for h in range(4):
                self.mm(self.PS[b][:, h * 128:(h + 1) * 128], B("kin")[:, h, :], B("qin")[:, h, :], r=[Kk("kin"), Kk("qin")], w=[("ps", b)])
            self.tt("dve", B("AT"), self.PS[b].rearrange("p (h c) -> p h c", h=4), bc_mid(triC, 4), ALU.mult, r=[("ps", b), tk], w=[Kk("AT")])

        def back(tok):
            d, si, i, n, first = tok
            t0, L = g.seqs[si]
            B = lambda nm_: ld[nm_][d][n]
            Kk = lambda nm_: (nm_, d, n)
            tok0 = t0 + i * 128
            ti = tok0 // 128
            Sd, Sbd = S[si][d], Sb[si][d]
            if not first:
                self.dma("sp", B("ofl"), g.of[ti], r=[("of", ti)], w=[Kk("ofl")], sem="x")
                self.dma("sp", B("got"), g.go_tok[ti], w=[Kk("got")], sem="x")
            bo = self.bank()
            for h in range(4):
                self.mm(self.PS[bo][:, h * 128:(h + 1) * 128], B("AT")[:, h, :], B("vt")[:, h * 128:(h + 1) * 128], start=True, stop=False, r=[Kk("AT"), Kk("vt")], w=[("ps", bo)])
                self.mm(self.PS[bo][:, h * 128:(h + 1) * 128], B("qin")[:, h, :], Sbd[:, h, :], start=False, stop=True, r=[Kk("qin"), ("Sb", si, d)], w=[("ps", bo)])
            bu = self.bank()
            for h in range(4):
                self.mm(self.PS[bu][:, h * 128:(h + 1) * 128], B("kdec")[:, h * 128:(h + 1) * 128], B("vt")[:, h * 128:(h + 1) * 128], r=[Kk("kdec"), Kk("vt")], w=[("ps", bu)])
            col = 127 if d == 0 else 0
            for h in range(4):
                self.stt("dve", Sd[:, h, :], Sd[:, h, :], B("Ep")[:, h, col:col + 1], self.PS[bu][:, h * 128:(h + 1) * 128], ALU.mult, ALU.add,
                         r=[("S", si, d), Kk("Ep"), ("ps", bu)], w=[("S", si, d)])
            self.cp("act", Sbd, Sd, r=[("S", si, d)], w=[("Sb", si, d)])
            ob = B("o")
            if first:
                self.cp("act", ob, self.PS[bo], r=[("ps", bo)], w=[Kk("o")])
                self.dma("pool", g.of[ti], ob, r=[Kk("o")], w=[("of", ti)], sem="x")
                return
            self.tt("dve", ob, self.PS[bo], B("ofl"), ALU.add, r=[("ps", bo), Kk("ofl")], w=[Kk("o")])
            for h in range(4):
                self.act(junk, ob[:, h * 128:(h + 1) * 128], AF.Square, r=[Kk("o")], w=["gjunk", ("ssq", h)], accum=ssq[:, h:h + 1])
            self.act(rs, ssq, AF.Sqrt, r=[("ssq", h) for h in range(4)] + ["eps"], w=["rs"], scale=1.0 / 128, bias=self.eps_t)
            self.recip(rs, rs, r=["rs"], w=["rs"])
            for h in range(4):
                self.stt("dve", gl[:, h * 128:(h + 1) * 128], ob[:, h * 128:(h + 1) * 128], rs[:, h:h + 1], self.g_gla_bc, ALU.mult, ALU.mult,
                         r=[Kk("o"), "rs", "g_gla_bc"], w=[("gl", h)])
            self.tt("pool", glb, gl, B("got"), ALU.mult, r=[("gl", h) for h in range(4)] + [Kk("got")], w=["glb"])
            b = self.bank()
            pb = self.PS[b].bitcast(BF16)
            for h in range(4):
                self.tr(pb[:, h * 128:(h + 1) * 128], glb[:, h * 128:(h + 1) * 128], self.ident_b, r=["glb", "ident_b"], w=[("ps", b)])
            self.cp("dve", glT, pb[:, 0:512].rearrange("p (h t) -> p h t", h=4), r=[("ps", b)], w=["glT"])
            tile_i = tok0 // TT
            qo = tok0 % TT
            self.dma("pool", g.mixT[tile_i, :, 4:8, qo:qo + 128], glT, r=["glT"], w=[("mixT", tile_i, "g", qo)], sem="x")

        for si, (t0, L) in enumerate(g.seqs):
            nt_ = L // 128
            chains = [[(0, si, i) for i in range(nt_)], [(1, si, i) for i in reversed(range(nt_))]]
            pend = [None, None]
            for k in range(nt_):
                toks = [f1(*chains[0][k]), f1(*chains[1][k])]
                if pend[0] is not None:
                    back(pend[0])
                f2(toks[0])
                f2(toks[1])
                if pend[1] is not None:
                    back(pend[1])
                f3(toks[0])
                f3(toks[1])
                pend = toks
            back(pend[0])
            back(pend[1])
        if g.prompt:
            for si in range(nseq):
                self.dma("pool", self.o_sf[si].rearrange("h k v -> k h v"), S[si][0], r=[("S", si, 0)], sem="x")
                self.dma("pool", self.o_sb[si].rearrange("h k v -> k h v"), S[si][1], r=[("S", si, 1)], sem="x")

    def phaseD(self, l):
        self.phase_begin()
        A = self.AR
        self.pool_banks = list(range(8))
        if l == 0:
            self.cast(self.wb_qkv, self.w_qkv, 8, 5120, "wb_qkv")
            self.cast(self.wb_o1, self.w_o1, 8, D, "wb_o1")
            self.cast(self.wb_f1[1], self.w_f1[1], 8, 2 * FH, "wb_f11")
            self.cast(self.wb_f2[1], self.w_f2[1], FC, D, "wb_f21")
        Wo = A.alloc([128, 8, D], BF16)
        wo_src = self.wb_o0 if l == 0 else self.wb_o1
        self.dma("sp", Wo, wo_src, w=["Wo"], sem="x")
        NS = 3
        W1 = [A.alloc([128, 8, 512], BF16) for _ in range(NS)]
        NW2 = 3
        W2 = [A.alloc([128, FC, 128], BF16) for _ in range(NW2)]
        xT = [A.alloc([128, DC, TT], F32) for _ in range(3)]
        hT = [A.alloc([128, DC, TT], BF16) for _ in range(2)]
        mix = A.alloc([128, DC, TT], BF16)
        osb = A.alloc([128, DC, TT], F32)
        sq = A.alloc([128, DC, TT], BF16)
        tmp = A.alloc([128, DC, TT], F32)
        rstd = A.alloc([128, TT], F32)
        actT = A.alloc([128, FC, TT], BF16)
        sg = [A.alloc([128, TT], BF16) for _ in range(2)]
        cnt = {"w1": 0, "w2": 0}
        items = [(g_, t_) for g_ in self.G for t_ in range(g_.nt)]
        NI = len(items)

        def stats(n_key):
            b = self.bank()
            for c in range(DC):
                self.mm(self.PS[b], self.ones_b, sq[:, c, :], start=(c == 0), stop=(c == DC - 1), r=[("sq", c), "ones_b"], w=[("ps", b)])
            self.rstd_from(rstd, self.PS[b], D, r=[("ps", b), "eps"], w=["rstd"])

        def residual(p, kind, gi_):
            for c in range(DC):
                self.stt("dve", tmp[:, c, :], osb[:, c, :], self.sc_a(l, gi_, kind, c), rstd, ALU.mult, ALU.mult,
                         r=[("osb", c), "rstd", ("sc", l, gi_, kind)], w=[("tmp", c)])
                self.tt("pool", xT[p][:, c, :], xT[p][:, c, :], tmp[:, c, :], ALU.add, r=[("xT", p, c), ("tmp", c)], w=[("xT", p, c)])

        def s1_load(k):
            g, t = items[k]
            p = k % 3
            self.dma("sp", xT[p], g.xT[t], w=[("xT", p, c) for c in range(DC)], sem="x")
            self.dma("sp", mix, g.mixT[t], w=["mix"], sem="x")

        def s1A(t):
            for oc in range(DC):
                b = self.bank()
                for kc in range(DC):
                    self.mm(self.PS[b], Wo[:, kc, oc * 128:(oc + 1) * 128], mix[:, kc, :], start=(kc == 0), stop=(kc == DC - 1), r=["Wo", "mix"], w=[("ps", b)])
                self.cp("act", osb[:, oc, :], self.PS[b], r=[("ps", b)], w=[("osb", oc)])
                self.tt("pool", sq[:, oc, :], osb[:, oc, :], osb[:, oc, :], ALU.mult, r=[("osb", oc)], w=[("sq", oc)])

        def s1B(t):
            stats(None)

        def s1C(k):
            p = k % 3
            residual(p, 1, items[k][0].gi)
            for c in range(DC):
                self.act(sq[:, c, :], xT[p][:, c, :], AF.Square, r=[("xT", p, c)], w=[("sq", c)])

        def s1D(t):
            stats(None)

        def s1E(k):
            p = k % 3
            ph = k % 2
            gi_ = items[k][0].gi
            for c in range(DC):
                self.stt("dve", tmp[:, c, :], xT[p][:, c, :], self.sc_a(l, gi_, 2, c), rstd, ALU.mult, ALU.mult,
                         r=[("xT", p, c), "rstd", ("sc", l, gi_, 2)], w=[("tmp", c)])
                self.act(hT[ph][:, c, :], tmp[:, c, :], AF.Identity, r=[("tmp", c), ("modpp", l)], w=[("hT", ph, c)],
                         bias=self.sc_shift(l, gi_, 3, c))

        def ffn_in_slab(k, j):
            p = k % 2
            s_ = cnt["w1"] % NS
            cnt["w1"] += 1
            self.dma("sp", W1[s_][:, :, 0:256], self.wb_f1[l][:, :, j * 256:(j + 1) * 256], w=[("W1", s_, 0)], sem="x")
            self.dma("sp", W1[s_][:, :, 256:512], self.wb_f1[l][:, :, FH + j * 256:FH + (j + 1) * 256], w=[("W1", s_, 1)], sem="x")
            for ii in range(2):
                i = j * 2 + ii
                bg = self.bank()
                for kc in range(DC):
                    self.mm(self.PS[bg], W1[s_][:, kc, ii * 128:(ii + 1) * 128], hT[p][:, kc, :], start=(kc == 0), stop=(kc == DC - 1), r=[("W1", s_, 0), ("hT", p, kc)], w=[("ps", bg)])
                bu = self.bank()
                for kc in range(DC):
                    self.mm(self.PS[bu], W1[s_][:, kc, 256 + ii * 128:256 + (ii + 1) * 128], hT[p][:, kc, :], start=(kc == 0), stop=(kc == DC - 1), r=[("W1", s_, 1), ("hT", p, kc)], w=[("ps", bu)])
                sgi = sg[i % 2]
                self.act(sgi, self.PS[bg], AF.Silu, r=[("ps", bg)], w=[("sg", i % 2)])
                self.tt("dve", actT[:, i, :], sgi, self.PS[bu], ALU.mult, r=[("sg", i % 2), ("ps", bu)], w=[("actT", i)])

        def ffn_out(t):
            for oc in range(DC):
                w_ = cnt["w2"] % NW2
                cnt["w2"] += 1
                self.dma("sp", W2[w_], self.wb_f2[l][:, :, oc * 128:(oc + 1) * 128], w=[("W2", w_)], sem="x")
                b = self.bank()
                for kc in range(FC):
                    self.mm(self.PS[b], W2[w_][:, kc, :], actT[:, kc, :], start=(kc == 0), stop=(kc == FC - 1), r=[("W2", w_), ("actT", kc)], w=[("ps", b)])
                self.cp("act", osb[:, oc, :], self.PS[b], r=[("ps", b)], w=[("osb", oc)])
                self.tt("pool", sq[:, oc, :], osb[:, oc, :], osb[:, oc, :], ALU.mult, r=[("osb", oc)], w=[("sq", oc)])

        def tailA(t):
            stats(None)

        def tailB(k):
            g, t = items[k]
            p = k % 3
            residual(p, 3, g.gi)
            if l == 0:
                self.dma("pool", g.xT[t], xT[p], r=[("xT", p, c) for c in range(DC)], w=[("xTs", g.name, t)], sem="x")

        def tailC(k):
            if l == 0:
                return
            g, t = items[k]
            p = k % 3
            yt = tmp.rearrange("p c t -> p (c t)").rearrange("p (s d) -> p s d", s=4)
            for s4 in range(4):
                for half in range(2):
                    b = self.bank()
                    for cc in range(4):
                        c = half * 4 + cc
                        self.tr(self.PS[b][:, cc * 128:(cc + 1) * 128], xT[p][:, c, s4 * 128:(s4 + 1) * 128], self.ident_f, r=[("xT", p, c), "ident_f"], w=[("ps", b)])
                    self.cp("act" if half else "dve", yt[:, s4, half * 512:(half + 1) * 512], self.PS[b], r=[("ps", b)], w=[("tmp", s4 * 2 + half)])
            self.dma("pool", g.y_out[t * TT:(t + 1) * TT, :].rearrange("(s p) d -> p s d", p=128), yt,
                     r=[("tmp", c) for c in range(DC)], sem="x")

        s1_load(0)
        s1A(0)
        s1B(0)
        s1C(0)
        s1D(0)
        s1E(0)
        for t in range(NI):
            hooks = {}
            if t > 0:
                hooks[0] = [lambda t=t: tailA(t - 1)]
                hooks[1] = [lambda t=t: tailB(t - 1)]
                hooks[4] = [lambda t=t: tailC(t - 1)]
            if t + 1 < NI:
                hooks.setdefault(2, []).append(lambda t=t: s1_load(t + 1))
                hooks.setdefault(3, []).append(lambda t=t: s1A(t + 1))
                hooks.setdefault(5, []).append(lambda t=t: s1B(t + 1))
                hooks.setdefault(6, []).append(lambda t=t: s1C(t + 1))
                hooks.setdefault(8, []).append(lambda t=t: s1D(t + 1))
                hooks.setdefault(9, []).append(lambda t=t: s1E(t + 1))
            for j in range(FC // 2):
                ffn_in_slab(t, j)
                for f in hooks.get(j, ()):
                    f()
            ffn_out(t)
        tailA(NI - 1)
        tailB(NI - 1)
        tailC(NI - 1)

    def phaseA1(self, g):
        self.phase_begin()
        A = self.AR
        l = 1
        W = A.alloc([128, 8, 5120], BF16)
        self.dma("sp", W, self.wb_qkv, w=["W"], sem="W")
        xT = A.alloc([128, DC, TT], F32)
        hTs = [A.alloc([128, DC, TT], BF16) for _ in range(2)]
        sq = A.alloc([128, DC, TT], BF16)
        tmp = A.alloc([128, DC, TT], F32)
        rstd = A.alloc([128, TT], F32)
        ropeC = A.alloc([128, TT], F32)
        ropeS = A.alloc([128, TT], F32)
        rtmp = A.alloc([128, 2, TT], F32)
        qk = [A.alloc([128, TT], BF16) for _ in range(3)]
        qbf = [A.alloc([128, TT], BF16) for _ in range(2)]
        vtok = [A.alloc([128, D], BF16) for _ in range(2)]
        ftok = [A.alloc([128, D], F32) for _ in range(2)]
        cx = A.alloc([128, D], F32)
        qi = 0
        if g.ctx:
            for s in range(g.ctx // 128):
                self.dma("sp", cx, self.dk_c[s * 128:(s + 1) * 128, :], w=["cx"], sem="cx")
                for half in range(2):
                    b = self.bank()
                    for cc in range(4):
                        c = half * 4 + cc
                        self.tr(self.PS[b][:, cc * 128:(cc + 1) * 128], cx[:, c * 128:(c + 1) * 128], self.ident_f, r=["cx", "ident_f"], w=[("ps", b)])
                    kq = qk[qi % 3]
                    kqk = ("qk", qi % 3)
                    qi += 1
                    self.cp("act" if half else "dve", kq, self.PS[b], r=[("ps", b)], w=[kqk])
                    self.dma("pool", g.Kf[half * 4:half * 4 + 4, :, s * 128:(s + 1) * 128].rearrange("h p t -> p h t"),
                             kq.rearrange("p (h t) -> p h t", h=4), r=[kqk], w=[("Kf", "c", s, half)], sem=f"qk{(qi - 1) % 3}")
                self.ck("b0")
                self.dma("pool", g.Vf[:, :, s, :].rearrange("h p e -> p h e"), self.dv_c[s * 128:(s + 1) * 128, :].rearrange("p (h e) -> p h e", h=8),
                         w=[("Vf", s)], sem="cast")
        def head(t):
            self.dma("sp", xT, g.xT[t], w=[("xT", c) for c in range(DC)], sem="xT_ld")
            self.norm_mod(xT, hTs[t % 2], l, g, 0, sq, rstd, tmp, [("xT", c) for c in range(DC)], hkeys=[("hT", t % 2, c) for c in range(DC)])

        def rope_load(t):
            if g.rope:
                self.dma("sp", ropeC, self.ropeD[0, :, t * TT:(t + 1) * TT], w=["ropeC"], sem="ropeC")
                self.dma("sp", ropeS, self.ropeD[1, :, t * TT:(t + 1) * TT], w=["ropeS"], sem="ropeS")

        head(0)
        for t in range(g.nt):
            hT = hTs[t % 2]
            hk = [("hT", t % 2, c) for c in range(DC)]
            rope_load(t)

            def projB(col0, b):
                for kc in range(DC):
                    self.mm(self.PS[b], W[:, kc, col0:col0 + 128], hT[:, kc, :], start=(kc == 0), stop=(kc == DC - 1), r=["W", hk[kc]], w=[("ps", b)])

            for which in range(2):
                base = 0 if which == 0 else 2048
                for h in range(8):
                    b = self.bank()
                    projB(base + h * 128, b)
                    o_ = qk[qi % 3]
                    ok = ("qk", qi % 3)
                    qi += 1
                    if g.rope:
                        qb_ = qbf[qi % 2]
                        self.cp("act", qb_, self.PS[b], r=[("ps", b)], w=[("qbf", qi % 2)])
                        b2 = self.bank()
                        self.mm(self.PS[b2], self.permB, qb_, r=["permB", ("qbf", qi % 2)], w=[("ps", b2)])
                        self.tt("dve", rtmp[:, 0, :], self.PS[b], ropeC, ALU.mult, r=[("ps", b), "ropeC"], w=[("rtmp", 0)])
                        self.tt("dve", rtmp[:, 1, :], self.PS[b2], ropeS, ALU.mult, r=[("ps", b2), "ropeS"], w=[("rtmp", 1)])
                        self.tt("pool", o_, rtmp[:, 0, :], rtmp[:, 1, :], ALU.add, r=[("rtmp", 0), ("rtmp", 1)], w=[ok])
                    else:
                        self.cp("act", o_, self.PS[b], r=[("ps", b)], w=[ok])
                    if which == 0:
                        self.dma("pool", g.Qf[h, :, t * TT:(t + 1) * TT], o_, r=[ok], w=[("Qf", h, t)], sem=f"qk{(qi - 1) % 3}")
                    else:
                        k0 = g.ctx + t * TT
                        self.dma("pool", g.Kf[h, :, k0:k0 + TT], o_, r=[ok], w=[("Kf", h, t)], sem=f"qk{(qi - 1) % 3}")
            if t + 1 < g.nt:
                head(t + 1)
            for s in range(4):
                tok0 = t * TT + s * 128
                kt = (g.ctx + tok0) // 128
                vt_ = vtok[s % 2]
                for half in range(2):
                    b = self.bank()
                    for kc in range(DC):
                        self.mm(self.PS[b], hT[:, kc, s * 128:(s + 1) * 128], W[:, kc, 4096 + half * 512:4096 + (half + 1) * 512], start=(kc == 0), stop=(kc == DC - 1), r=["W", hk[kc]], w=[("ps", b)])
                    self.cp("act" if half else "dve", vt_[:, half * 512:(half + 1) * 512], self.PS[b], r=[("ps", b)], w=[("vtok", s % 2, half)])
                    if g.prompt:
                        ft = ftok[0]
                        self.cp("dve" if half else "act", ft[:, half * 512:(half + 1) * 512], self.PS[b], r=[("ps", b)], w=[("ftok", 0, half)])
                self.dma("pool", g.Vf[:, :, kt, :].rearrange("h p e -> p h e"), vt_.rearrange("p (h e) -> p h e", h=8), r=[("vtok", s % 2, 0), ("vtok", s % 2, 1)], w=[("Vf", kt)], sem=f"vtok{s % 2}")
                if g.prompt:
                    self.dma("pool", self.o_dv[tok0:tok0 + 128, :], ftok[0], r=[("ftok", 0, 0), ("ftok", 0, 1)], sem="ftok0")
                    ft = ftok[1]
                    for half in range(2):
                        b = self.bank()
                        for kc in range(DC):
                            self.mm(self.PS[b], hT[:, kc, s * 128:(s + 1) * 128], W[:, kc, 2048 + half * 512:2048 + (half + 1) * 512], start=(kc == 0), stop=(kc == DC - 1), r=["W", hk[kc]], w=[("ps", b)])
                        self.cp("act" if half else "dve", ft[:, half * 512:(half + 1) * 512], self.PS[b], r=[("ps", b)], w=[("ftok", 1, half)])
                    self.dma("pool", self.o_dk[tok0:tok0 + 128, :], ft, r=[("ftok", 1, 0), ("ftok", 1, 1)], sem="ftok1")
                self.ck("b3")
            if g.prompt:
                self.ck("b5")
        self.ck("b4")


def rope_tables(n_tok, rot_dim, grid_w=64, base=10000.0):
    rows = n_tok // grid_w
    row = np.repeat(np.arange(rows, dtype=np.float64), grid_w)
    col = np.tile(np.arange(grid_w, dtype=np.float64), rows)
    n_freq = rot_dim // 4
    inv = (np.float32(base) ** (-np.arange(n_freq, dtype=np.float32) / np.float32(n_freq))).astype(np.float64)
    ang = np.concatenate([row[:, None] * inv, col[:, None] * inv], axis=-1)
    ang = ang.astype(np.float32).astype(np.float64)
    cos, sin = np.cos(ang), np.sin(ang)
    C = np.repeat(cos, 2, axis=1).T
    S = np.repeat(sin, 2, axis=1).T.copy()
    S[0::2] *= -1.0
    return np.stack([C, S]).astype(np.float32)


def pair_swap_idx(n):
    idx = np.arange(n)
    return idx ^ 1


def prep_shared(inp, cfg):
    f = lambda a: np.ascontiguousarray(np.asarray(a, dtype=np.float32))
    sh = {}
    sh["w_mod"] = f(inp["w_mod"])
    sh["b_mod"] = f(inp["b_mod"])
    w = np.asarray(inp["ab_w_in"], np.float32)[0]
    q_lat, kv_lat, k_rope = w[:, 0:384], w[:, 384:640], w[:, 640:672]
    gq, gk, gv, gg, go = w[:, 672:1184], w[:, 1184:1696], w[:, 1696:2208], w[:, 2208:2240], w[:, 2240:2752]
    ggl = np.concatenate([gg[:, 0:16], gg[:, 0:16], gg[:, 16:32], gg[:, 16:32]], axis=1)
    sh["w_in"] = f(np.concatenate([q_lat, k_rope, k_rope[:, pair_swap_idx(32)], ggl, gq, gk, kv_lat, k_rope, gk, gv, go], axis=1))
    assert sh["w_in"].shape[1] == NIN
    sh["g_kv"] = f(inp["mla_g_kv"][0])
    uq = np.asarray(inp["mla_w_uq"], np.float32)[0]
    idx = np.arange(768).reshape(8, 96).copy()
    idx[:, 64:] = idx[:, 64:] ^ 1
    sh["w_uq"] = f(np.concatenate([uq, uq[:, idx.reshape(-1)]], axis=1))
    ukv = np.asarray(inp["mla_w_ukv"], np.float32)[0].reshape(256, 8, 128)
    sh["w_ukv"] = f(np.concatenate([ukv[:, :, 0:64].reshape(256, 512), ukv[:, :, 64:128].reshape(256, 512)], axis=1))
    sh["w_up"] = f(inp["gla_w_gate_up"][0])
    sh["b_up"] = f(inp["gla_b_gate"][0])
    sh["g_gla"] = f(inp["gla_g_out"][0])
    sh["w_o0"] = f(inp["ab_w_out"][0])
    qkv = np.asarray(inp["c_w_qkv"], np.float32)[0]
    q, k, v = qkv[:, 0:1024], qkv[:, 1024:2048], qkv[:, 2048:3072]
    sw = pair_swap_idx(1024)
    sh["w_qkv"] = f(np.concatenate([q, q[:, sw], k, k[:, sw], v], axis=1))
    sh["dlam"] = f(np.asarray(inp["diff_lambda"], np.float32)[0].reshape(-1))
    sh["w_o1"] = f(inp["c_w_out"][0])
    sh["w_f1"] = f(inp["w_ffn_in"])
    sh["w_f2"] = f(inp["w_ffn_out"])
    sh["ropeM"] = rope_tables(cfg.LS, 32)
    sh["ropeD"] = np.ascontiguousarray(np.tile(rope_tables(cfg.LS, 64), (1, 2, 1)))
    pm = np.zeros((128, 128), np.float32)
    pm[np.arange(128) ^ 1, np.arange(128)] = 1.0
    sh["perm"] = pm
    return sh


def core_inputs(inp, cfg, core, sh):
    f = lambda a: np.ascontiguousarray(np.asarray(a, dtype=np.float32))
    m = dict(sh)
    m["x_s"] = f(inp["x_sample"][core])
    m["x_p"] = f(np.asarray(inp["x_prompt"])[cfg.NPS * core:cfg.NPS * (core + 1)].reshape(cfg.NPS * cfg.LP, D))
    m["ckv_c"] = f(inp["cache_mla_ckv"][core, 0])
    m["kr_c"] = f(inp["cache_mla_krope"][core, 0])
    m["sf_in"] = f(inp["state_gla_fwd"][core, 0])
    m["sb_in"] = f(inp["state_gla_bwd"][core, 0])
    m["dk_c"] = f(np.asarray(inp["cache_diff_k"])[core, 0].reshape(cfg.CTX, D))
    m["dv_c"] = f(np.asarray(inp["cache_diff_v"])[core, 0].reshape(cfg.CTX, D))
    sv = np.zeros((84, 128), np.float32)
    sv[0:64] = np.asarray(inp["g_norm"], np.float32).reshape(64, 128)
    sv[64:67] = np.asarray(inp["mla_g_q"], np.float32)[0].reshape(3, 128)
    sv[67] = np.asarray(inp["diff_g_out"], np.float32)[0]
    sv[68:76] = np.asarray(inp["c"], np.float32)[core].reshape(8, 128)
    sv[76:84] = np.asarray(inp["c_ctx"], np.float32).reshape(8, 128)
    m["smallv"] = sv
    return m


_NC_CACHE = {}


def run(inp, cfg, n_cores, stop_after=None, trace=False):
    key = (cfg.LS, cfg.CTX, cfg.LP, cfg.NPS, stop_after)
    if key not in _NC_CACHE:
        _NC_CACHE[key] = K(cfg, stop_after).build()
    nc = _NC_CACHE[key]
    sh = prep_shared(inp, cfg)
    in_maps = [core_inputs(inp, cfg, c, sh) for c in range(n_cores)]
    res = run_bass_kernel_spmd(nc, in_maps, core_ids=list(range(n_cores)), **({"trace": True} if trace else {}))
    R = res.results
    NPS, LP = cfg.NPS, cfg.LP
    cat = lambda k: np.concatenate([np.asarray(r[k]) for r in R], axis=0)
    y_p = cat("y_p").reshape(n_cores * NPS, LP, D)
    y_s = np.stack([np.asarray(r["y_s"]) for r in R])
    ckv = cat("o_ckv").reshape(n_cores * NPS, 1, LP, 256)
    kr = cat("o_kr").reshape(n_cores * NPS, 1, LP, 32)
    sf = cat("o_sf").reshape(n_cores * NPS, 1, 4, 128, 128)
    sb = cat("o_sb").reshape(n_cores * NPS, 1, 4, 128, 128)
    dk = cat("o_dk").reshape(n_cores * NPS, 1, LP, 8, 2, 64)
    dv = cat("o_dv").reshape(n_cores * NPS, 1, LP, 8, 128)
    outs = tuple(np.ascontiguousarray(a.astype(np.float32)) for a in (y_p, y_s, ckv, kr, sf, sb, dk, dv))
    return outs, res


def kernel(**inputs):
    cfg = Cfg()
    outs, _ = run(inputs, cfg, 8)
    return outs
```
